# Optimizing a Trainium2 kernel written in Bass

```python
import jax, jax.numpy as jnp
from jax import lax
import numpy as np

D_MODEL = 1024
BATCH = 16
SEQ = 2048
DEPTH = 1
DEC_BATCH = 128
DEC_SEQ = 4
PAST_LEN = 8192
PAGE_SIZE = 128

HEAD_DIM = 64
ATT_HEADS = 8
ATT_WIDTH = ATT_HEADS * HEAD_DIM
CONV_WIDTH = D_MODEL - ATT_WIDTH
CONV_K = 3
DILATED_PATTERNS = ((128, 1), (512, 4), (2048, 16))
MAX_SPAN = max(w for w, _ in DILATED_PATTERNS)
D_FF = 4 * D_MODEL
ROPE_THETA = 10000.0
NORM_EPS = 1e-6
Q_BLOCK = 128
PROJ_COLS = 3 * ATT_WIDTH + 3 * CONV_WIDTH
PROJ_SPLITS = (ATT_WIDTH, 2 * ATT_WIDTH, 3 * ATT_WIDTH,
               3 * ATT_WIDTH + CONV_WIDTH, 3 * ATT_WIDTH + 2 * CONV_WIDTH)

kernel_name = "hymba_dilated_attn_shortconv_decoder_step"


def rmsnorm(x, g):
    xf = x.astype(jnp.float32)
    y = xf * lax.rsqrt(jnp.mean(xf * xf, axis=-1, keepdims=True) + NORM_EPS)
    return (y * g.astype(jnp.float32)).astype(x.dtype)


def rope(x, pos):
    half = x.shape[-1] // 2
    inv = ROPE_THETA ** (-jnp.arange(half, dtype=jnp.float32) * 2.0 / x.shape[-1])
    ang = pos.astype(jnp.float32)[:, None] * inv[None, :]
    cos = jnp.cos(ang)[None, :, None, :]
    sin = jnp.sin(ang)[None, :, None, :]
    xf = x.astype(jnp.float32)
    x1, x2 = xf[..., :half], xf[..., half:]
    return jnp.concatenate([x1 * cos - x2 * sin, x2 * cos + x1 * sin], axis=-1).astype(x.dtype)


def dilated_attention_block(q, pos, k_all, v_all, row_offset):
    neg = jnp.finfo(jnp.float32).min
    n_rows = k_all.shape[1]
    qf = q.astype(jnp.float32) * (HEAD_DIM ** -0.5)
    outs, lses = [], []
    for window, dil in DILATED_PATTERNS:
        dist = jnp.arange(window // dil + 1) * dil
        kpos = pos[:, None] - dist[None, :]
        valid = kpos >= 0
        rows = jnp.clip(kpos - row_offset, 0, n_rows - 1)
        kg = jnp.take(k_all, rows, axis=1).astype(jnp.float32)
        vg = jnp.take(v_all, rows, axis=1).astype(jnp.float32)
        s = jnp.einsum('bqhd,bqjhd->bqhj', qf, kg)
        s = jnp.where(valid[None, :, None, :], s, neg)
        m = jnp.max(s, axis=-1, keepdims=True)
        p = jnp.exp(s - m)
        den = jnp.sum(p, axis=-1, keepdims=True)
        outs.append(jnp.einsum('bqhj,bqjhd->bqhd', p, vg) / den)
        lses.append(m + jnp.log(den))
    w = jax.nn.softmax(jnp.concatenate(lses, axis=-1), axis=-1)
    out = w[..., 0:1] * outs[0] + w[..., 1:2] * outs[1] + w[..., 2:3] * outs[2]
    return out.astype(q.dtype)


def parallel_mixers(h, start, k_past, v_past, conv_past, w_in, conv_w, g_att, g_conv, w_out):
    B, T, _ = h.shape
    pos = start + jnp.arange(T)
    proj = h @ w_in
    q, k, v, gate_b, gate_c, u = jnp.split(proj, PROJ_SPLITS, axis=-1)
    q = rope(q.reshape(B, T, ATT_HEADS, HEAD_DIM), pos)
    k = rope(k.reshape(B, T, ATT_HEADS, HEAD_DIM), pos)
    v = v.reshape(B, T, ATT_HEADS, HEAD_DIM)

    if k_past is None:
        n_blk = T // Q_BLOCK
        qb = q.reshape(B, n_blk, Q_BLOCK, ATT_HEADS, HEAD_DIM).swapaxes(0, 1)
        pb = pos.reshape(n_blk, Q_BLOCK)
        ob = lax.map(lambda a: dilated_attention_block(a[0], a[1], k, v, 0), (qb, pb))
        attn = ob.swapaxes(0, 1).reshape(B, T, ATT_WIDTH)
        conv_pad = jnp.zeros((B, CONV_K - 1, CONV_WIDTH), h.dtype)
    else:
        k_all = jnp.concatenate([k_past, k], axis=1)
        v_all = jnp.concatenate([v_past, v], axis=1)
        attn = dilated_attention_block(q, pos, k_all, v_all, start - k_past.shape[1])
        attn = attn.reshape(B, T, ATT_WIDTH)
        conv_pad = conv_past

    gated_in = gate_c * u
    u_pad = jnp.concatenate([conv_pad.astype(gated_in.dtype), gated_in], axis=1)
    conv = conv_w[0] * u_pad[:, 0:T]
    for tap in range(1, CONV_K):
        conv = conv + conv_w[tap] * u_pad[:, tap:tap + T]
    conv_out = gate_b * conv
    conv_state = u_pad[:, -(CONV_K - 1):]

    mixed = jnp.concatenate([rmsnorm(attn, g_att), rmsnorm(conv_out, g_conv)], axis=-1) @ w_out
    return mixed, k, v, conv_state


def decoder_layer(x, start, k_past, v_past, conv_past, n_att_pre, n_att_post, w_in, conv_w,
                  g_att, g_conv, w_out, n_mlp_pre, n_mlp_post, w_up, w_down):
    h = rmsnorm(x, n_att_pre)
    mixed, k, v, conv_state = parallel_mixers(h, start, k_past, v_past, conv_past,
                                              w_in, conv_w, g_att, g_conv, w_out)
    x = x + rmsnorm(mixed, n_att_post)
    h = rmsnorm(x, n_mlp_pre)
    f = jnp.square(jax.nn.relu(h @ w_up)) @ w_down
    x = x + rmsnorm(f, n_mlp_post)
    return x, k, v, conv_state


def setup_inputs(seed: int = 0) -> dict:
    key = jax.random.key(seed)
    ks = jax.random.split(key, 20)
    lw = min(MAX_SPAN, PAST_LEN)
    f32 = jnp.float32
    nrm = lambda k, shape, s: jax.random.normal(k, shape, f32) * s
    gain = lambda k, shape: 1.0 + 0.05 * jax.random.normal(k, shape, f32)
    return {
        "x_prompt": nrm(ks[0], (BATCH, SEQ, D_MODEL), 1.0),
        "x_sample": nrm(ks[1], (DEC_BATCH, DEC_SEQ, D_MODEL), 1.0),
        "cache_k": nrm(ks[2], (DEPTH, DEC_BATCH, lw, ATT_HEADS, HEAD_DIM), 1.0),
        "cache_v": nrm(ks[3], (DEPTH, DEC_BATCH, lw, ATT_HEADS, HEAD_DIM), 1.0),
        "state_conv": nrm(ks[4], (DEPTH, DEC_BATCH, CONV_K - 1, CONV_WIDTH), 1.0),
        "n_att_pre": gain(ks[5], (DEPTH, D_MODEL)),
        "n_att_post": gain(ks[6], (DEPTH, D_MODEL)),
        "w_in": nrm(ks[7], (DEPTH, D_MODEL, PROJ_COLS), D_MODEL ** -0.5),
        "conv_w": nrm(ks[8], (DEPTH, CONV_K, CONV_WIDTH), CONV_K ** -0.5),
        "g_att": gain(ks[9], (DEPTH, ATT_WIDTH)),
        "g_conv": gain(ks[10], (DEPTH, CONV_WIDTH)),
        "w_out": nrm(ks[11], (DEPTH, D_MODEL, D_MODEL), D_MODEL ** -0.5),
        "n_mlp_pre": gain(ks[12], (DEPTH, D_MODEL)),
        "n_mlp_post": gain(ks[13], (DEPTH, D_MODEL)),
        "w_up": nrm(ks[14], (DEPTH, D_MODEL, D_FF), D_MODEL ** -0.5),
        "w_down": nrm(ks[15], (DEPTH, D_FF, D_MODEL), D_FF ** -0.5),
    }


def reference(x_prompt, x_sample, cache_k, cache_v, state_conv, n_att_pre, n_att_post, w_in,
              conv_w, g_att, g_conv, w_out, n_mlp_pre, n_mlp_post, w_up, w_down):
    keep_p = min(MAX_SPAN, x_prompt.shape[1])
    y_p, y_s = x_prompt, x_sample
    kp_l, vp_l, cp_l, ks_l, vs_l, cs_l = [], [], [], [], [], []
    for l in range(DEPTH):
        params = (n_att_pre[l], n_att_post[l], w_in[l], conv_w[l], g_att[l], g_conv[l],
                  w_out[l], n_mlp_pre[l], n_mlp_post[l], w_up[l], w_down[l])
        y_p, kp, vp, cp = decoder_layer(y_p, 0, None, None, None, *params)
        y_s, ks, vs, cs = decoder_layer(y_s, PAST_LEN, cache_k[l], cache_v[l], state_conv[l], *params)
        kp_l.append(kp[:, -keep_p:])
        vp_l.append(vp[:, -keep_p:])
        cp_l.append(cp)
        ks_l.append(ks)
        vs_l.append(vs)
        cs_l.append(cs)
    return (y_p, y_s, jnp.stack(kp_l), jnp.stack(vp_l), jnp.stack(cp_l),
            jnp.stack(ks_l), jnp.stack(vs_l), jnp.stack(cs_l))
```

```python
import numpy as np
from contextlib import ExitStack
import concourse.bass as bass
import concourse.mybir as mybir
from concourse.bass_utils import run_bass_kernel_spmd

F32 = mybir.dt.float32
BF16 = mybir.dt.bfloat16
AF = mybir.ActivationFunctionType
ALU = mybir.AluOpType

NCORES = 8
D = 1024
SEQ = 2048
NSEQ = 2
NS = 16
PAST = 8192
EPS = 1e-6
NEG = -30000.0
ENGS = ("pe", "act", "dve", "pool", "sp")


class Res:
    __slots__ = ("name", "w", "r")

    def __init__(self, name):
        self.name = name
        self.w = None
        self.r = []


class Sched:
    def __init__(self, ndma=20):
        self.ops = {e: [] for e in ENGS}
        self.known = {e: {} for e in ENGS}
        self.ndma = ndma
        self.dma_cnt = {("sp", k): 0 for k in range(ndma)}
        self.dma_cnt.update({("pool", k): 0 for k in range(ndma)})
        self.dma_rr = {"sp": 0, "pool": 0}
        self.dma_last = {}

    def op(self, eng, fn, reads=(), writes=(), dma=False):
        deps = []
        for r in reads:
            if r.w is not None:
                deps.append(r.w)
        for w in writes:
            if w.w is not None:
                deps.append(w.w)
            deps.extend(w.r)
        known = self.known[eng]
        idx = len(self.ops[eng])
        dkey = None
        if dma:
            k = self.dma_rr[eng]
            self.dma_rr[eng] = (k + 1) % self.ndma
            dkey = (eng, k)
            if dkey in self.dma_last:
                deps.append(self.dma_last[dkey])
        waits = []
        newknown = None
        for (key, seq, vc) in deps:
            cur = (newknown if newknown is not None else known)
            if cur.get(key, 0) >= seq:
                continue
            if key == "pe" and eng == "pe" and not dma:
                continue
            if newknown is None:
                newknown = dict(known)
            waits.append((key, seq))
            for kk, vv in vc.items():
                if newknown.get(kk, 0) < vv:
                    newknown[kk] = vv
            if newknown.get(key, 0) < seq:
                newknown[key] = seq
        if newknown is not None:
            self.known[eng] = newknown
            known = newknown
        wmax = {}
        for key, seq in waits:
            if wmax.get(key, 0) < seq:
                wmax[key] = seq
        rec = {"fn": fn, "waits": list(wmax.items()), "sig": False, "dma": dkey, "dseq": None}
        self.ops[eng].append(rec)
        if dma:
            self.dma_cnt[dkey] += 1
            rec["dseq"] = self.dma_cnt[dkey]
            tok = (dkey, rec["dseq"], known)
            self.dma_last[dkey] = tok
        else:
            tok = (eng, idx + 1, known)
        for r in reads:
            r.r.append(tok)
        for w in writes:
            w.w = tok
            w.r = []
        return tok

    def emit(self, nc, block, sems, dsems):
        for e in ENGS:
            for rec in self.ops[e]:
                for key, seq in rec["waits"]:
                    if isinstance(key, str):
                        self.ops[key][seq - 1]["sig"] = True
        sigval = {}
        for e in ENGS:
            c = 0
            for i, rec in enumerate(self.ops[e]):
                if rec["sig"]:
                    c += 1
                    sigval[(e, i + 1)] = c
        import sys
        print("SCHED ops", {e: len(self.ops[e]) for e in ENGS}, "signals",
              {e: sum(1 for r in self.ops[e] if r["sig"]) for e in ENGS},
              "waits", {e: sum(len(r["waits"]) for r in self.ops[e]) for e in ENGS}, file=sys.stderr)
        finals = []
        for dkey, cnt in self.dma_cnt.items():
            if cnt > 0:
                finals.append((dkey, cnt))

        def run(e, eng):
            for rec in self.ops[e]:
                for key, seq in rec["waits"]:
                    if isinstance(key, str):
                        eng.wait_ge(sems[key], sigval[(key, seq)])
                    else:
                        eng.wait_ge(dsems[key], 16 * seq)
                ins = rec["fn"](eng)
                if rec["dma"] is not None:
                    ins.then_inc(dsems[rec["dma"]], 16)
                elif rec["sig"]:
                    ins.then_inc(sems[e], 1)
            if e == "pool":
                for dkey, cnt in finals:
                    eng.wait_ge(dsems[dkey], 16 * cnt)

        block.tensor(lambda eng: run("pe", eng))
        block.scalar(lambda eng: run("act", eng))
        block.vector(lambda eng: run("dve", eng))
        block.gpsimd(lambda eng: run("pool", eng))
        block.sync(lambda eng: run("sp", eng))


def bcast_mid(ap2d, n):
    a = ap2d.ap
    return bass.AP(ap2d.tensor, ap2d.offset, [list(a[0]), [0, n], list(a[1])])


def bcast_last(ap2d, n):
    a = ap2d.ap
    return bass.AP(ap2d.tensor, ap2d.offset, [list(a[0]), list(a[1]), [0, n]])


def build(with_sample=True, dbg=None):
    nc = bass.Bass("TRN2", target_bir_lowering=False)

    def din(name, shape, dt=F32):
        return nc.dram_tensor(name, list(shape), dt, kind="ExternalInput").ap()

    def dout(name, shape, dt=F32):
        return nc.dram_tensor(name, list(shape), dt, kind="ExternalOutput").ap()

    def dscr(name, shape, dt):
        return nc.dram_tensor(name, list(shape), dt, kind="Internal").ap()

    xp = din("xp", [NSEQ, SEQ, D])
    xs = din("xs", [64, D])
    ck = din("ck", [NS, 2048, 512])
    cv = din("cv", [NS, 2048, 512])
    scv = din("scv", [NS, 2, 512])
    w_in = din("w_in", [D, 3072])
    w_out = din("w_out", [D, D])
    w_up = din("w_up", [D, 4096])
    w_down = din("w_down", [4096, D])
    gpre_d = din("gpre", [128, 16])
    gpost_d = din("gpost", [128, 2048])
    gmix_d = din("gmix", [128, 8])
    convw_d = din("convw", [128, 12])
    ident_d = din("ident", [128, 128])
    mask12_d = din("mask12", [128, 3 * 512])
    mask3_d = din("mask3", [128, 4 * 512])
    csp_d = din("csp", [128, 16 * 64])
    css_d = din("css", [64, 64])
    smask_d = din("smask", [128, 32])
    wnew_d = din("wnew", [64, 512])

    yp = dout("yp", [NSEQ, SEQ, D])
    ys = dout("ys", [64, D])
    kp = dout("kp", [NSEQ, SEQ, 512])
    vp = dout("vp", [NSEQ, SEQ, 512])
    cp = dout("cp", [NSEQ, 2, 512])
    ks = dout("ks", [64, 512])
    vs = dout("vs", [64, 512])
    cso = dout("cso", [NS, 2, 512])
    dbgo = dout("dbgo", [128, 1024]) if dbg is not None else None

    vscr = dscr("vscr", [NSEQ, SEQ, 768], BF16)
    wup_s = dscr("wup_s", [D, 4096], BF16)
    wdn_s = dscr("wdn_s", [4096, D], BF16)

    S = Sched()
    es = ExitStack()
    res = {}

    def R(name):
        if name not in res:
            res[name] = Res(name)
        return res[name]

    def sb(name, shape, dt):
        return es.enter_context(nc.sbuf_tensor(name, list(shape), dt))

    w_in_sb = sb("w_in_sb", [128, 8, 3072], BF16)
    w_out_sb = sb("w_out_sb", [128, 8, 1024], BF16)
    gpre = sb("gpre_sb", [128, 2, 8], F32)
    gpost = sb("gpost_sb", [128, 2, 1024], F32)
    gmix = sb("gmix_sb", [128, 2, 4], F32)
    convw = sb("convw_sb", [128, 4, 3], F32)
    ident_bf = sb("ident_bf", [128, 128], BF16)
    ident_f = sb("ident_f", [128, 128], F32)
    ones_bf = sb("ones_bf", [128, 128], BF16)
    mask12 = sb("mask12_sb", [128, 3, 512], BF16)
    mask3 = sb("mask3_sb", [128, 4, 512], BF16)
    csg = sb("csg", [128, 4, 64], F32)
    kT = sb("kT", [128, 4, 2048], BF16)
    qT = sb("qT", [128, 4, 512], BF16)
    Vnat = sb("Vnat", [128, 8, 768], BF16)
    Vr4 = sb("Vr4", [128, 8, 768], BF16)
    Vr16 = sb("Vr16", [128, 16, 768], BF16)
    convT = sb("convT", [128, 4, 512], BF16)
    attnT = sb("attnT", [128, 4, 512], BF16)
    small = sb("small", [128, 16], F32)
    s1 = sb("s1", [128, 2048], F32)
    s2 = sb("s2", [128, 2048], F32)
    s3 = sb("s3", [128, 2064], F32)
    s4 = sb("s4", [128, 1024], F32)
    s5 = sb("s5", [128, 1024], F32)
    s6 = sb("s6", [128, 1024], F32)
    s9 = sb("s9", [128, 1024], F32)
    junk = sb("junk", [128, 1024], BF16)
    cbuf = sb("cbuf", [128, 1024], F32)
    carry = sb("carry", [128, 4, 2], F32)
    epsc = sb("epsc", [128, 1], F32)
    P_s = sb("P_s", [128, 8, 12], BF16)
    P_s2 = sb("P_s2", [128, 8, 12], BF16)
    P_new = sb("P_new", [64, 8, 64], BF16)
    smask8 = sb("smask8", [128, 8, 4], BF16)
    wnew8 = sb("wnew8", [64, 8, 64], BF16)

    xring = [s1[:, 0:1024], s1[:, 1024:2048]]
    x1buf = [s1[:, 0:1024], s1[:, 1024:2048]]
    hT = s2[:, :].bitcast(BF16).rearrange("p (c n) -> p c n", c=8)
    wu_ring = [s2[:, 0:1024].bitcast(BF16).rearrange("p (c n) -> p c n", c=8),
               s2[:, 1024:2048].bitcast(BF16).rearrange("p (c n) -> p c n", c=8)]
    gi = s3[:, :].rearrange("p (j n) -> p j n", j=4)
    wd_ring = [s3[:, 0:1024].bitcast(BF16).rearrange("p (k n) -> p k n", k=2),
               s3[:, 1024:2048].bitcast(BF16).rearrange("p (k n) -> p k n", k=2)]
    hbf = s4[:, 0:512].bitcast(BF16)
    q_r = s4[:, 512:768].bitcast(BF16)
    k_rb = s4[:, 768:1024].bitcast(BF16)
    accs = [s4[:, 0:512], s4[:, 512:1024]]
    tmpC = s4[:, :]
    k_r = s5[:, 0:512]
    v_f = s5[:, 512:1024]
    Rb = s5[:, 0:512]
    rbB = s5[:, 512:1024]
    xr = s5[:, :]
    ropeA = s6[:, 0:256]
    ropeB = s6[:, 256:512]
    u_sb = s6[:, 512:1024]
    Pb = [s6[:, 0:256].bitcast(BF16), s6[:, 256:512].bitcast(BF16)]
    sqB = s6[:, 512:768].bitcast(BF16)
    h2 = s6[:, 0:512].bitcast(BF16)
    r_ring = [s6[:, 512:640].bitcast(BF16), s6[:, 640:768].bitcast(BF16)]
    r2_ring = [s6[:, 768:896].bitcast(BF16), s6[:, 896:1024].bitcast(BF16)]
    t0 = s9[:, 0:512]
    sqA = s9[:, 512:768].bitcast(BF16)
    rbA = s9[:, 512:1024]
    h2T = s9[:, 0:1024].bitcast(BF16).rearrange("p (c n) -> p c n", c=8)

    r_hT = [R("wu0"), R("wu1")]
    r_gi = [R("wd0"), R("wd1")]
    q0, q1, q2, q3 = R("s4q0"), R("s4q1"), R("s4q2"), R("s4q3")
    r_hbf, r_qr, r_krb = [q0, q1], [q2], [q3]
    r_acc = [[q0, q1], [q2, q3]]
    r_tmpC = [q0, q1, q2, q3]
    r_kr, r_vf = [R("s5a")], [R("s5b")]
    r_xr = [R("s5a"), R("s5b")]
    ee = [R(f"s6e{i}") for i in range(8)]
    r_ropeA, r_ropeB, r_usb = ee[0:2], ee[2:4], ee[4:8]
    r_P = [ee[0:2], ee[2:4]]
    r_sqB, r_h2 = ee[4:6], ee[0:4]
    r_r = [[ee[4]], [ee[5]]]
    r_r2 = [[ee[6]], [ee[7]]]
    r_t0, r_s9b = [R("s9a")], [R("s9b")]
    r_h2T = [R("s9a"), R("s9b")]
    sqB2 = cbuf[:, 0:256].bitcast(BF16)
    Pb4 = [Pb[0], Pb[1], cbuf[:, 512:768].bitcast(BF16), cbuf[:, 768:1024].bitcast(BF16)]
    r_P4 = [r_P[0], r_P[1], [R("cbufb")], [R("cbufb2")]]
    SBK = [0, 1, 5, 6]
    r_attnAll = [R("attnT")] + [R(f"attnT{i}") for i in range(4)]
    r_convAll = [R("convT")] + [R(f"convT{i}") for i in range(4)]
    pp = [es.enter_context(nc.psum_tensor(f"pp{i}", [128, 1024], F32)) for i in range(4)]

    def bank(k):
        return pp[k // 2][:, (k % 2) * 512:(k % 2) * 512 + 512]

    def RB(k):
        return R(f"bank{k}")

    S.op("pool", lambda e: e.dma_start(out=ident_bf[:, :], in_=ident_d), writes=[R("ident_bf")], dma=True)
    S.op("pool", lambda e: e.dma_start(out=mask12[:, :, :], in_=mask12_d.rearrange("p (v n) -> p v n", v=3)),
         writes=[R("mask12")], dma=True)
    S.op("pool", lambda e: e.dma_start(out=mask3[:, :, :], in_=mask3_d.rearrange("p (v n) -> p v n", v=4)),
         writes=[R("mask3")], dma=True)
    w_in_v = w_in.rearrange("(c p) n -> p c n", p=128)
    for j in range(6):
        S.op("pool", lambda e, j=j: e.dma_start(out=w_in_sb[:, :, j * 512:(j + 1) * 512],
                                                 in_=w_in_v[:, :, j * 512:(j + 1) * 512]),
             writes=[R(f"w_in{j}")], dma=True)
    w_out_v = w_out.rearrange("(c p) n -> p c n", p=128)
    S.op("pool", lambda e: e.dma_start(out=w_out_sb[:, :, :], in_=w_out_v), writes=[R("w_out")], dma=True)
    stg_aps = [Vr4[:, 4:8, :].rearrange("p t n -> p (t n)")[:, 0:2048],
               Vnat[:, 4:8, :].rearrange("p t n -> p (t n)")[:, 0:2048]]
    stg_rs = [[R("stgA")], [R("stgB")]]
    wup_v = w_up.rearrange("(c p) n -> p c n", p=128)
    wups_v = wup_s.rearrange("(c p) n -> p c n", p=128)
    wdn_v = w_down.rearrange("(k p) n -> p k n", p=128)
    wdns_v = wdn_s.rearrange("(k p) n -> p k n", p=128)
    r_wconv = [R(f"wconv{q}") for q in range(32)]
    cstate = {"i": 0, "done": False}

    def pump(n=1):
        for _ in range(n):
            q = cstate["i"]
            if q >= 32:
                return
            cstate["i"] += 1
            sl = q % 2
            if q < 16:
                src = wup_v[:, :, q * 256:(q + 1) * 256]
                dst = wups_v[:, :, q * 256:(q + 1) * 256]
                sv = stg_aps[sl].rearrange("p (c n) -> p c n", c=8)
            else:
                k = q - 16
                src = wdn_v[:, 2 * k:2 * k + 2, :]
                dst = wdns_v[:, 2 * k:2 * k + 2, :]
                sv = stg_aps[sl].rearrange("p (k n) -> p k n", k=2)
            if cstate.get("pend") is not None:
                cstate["pend"]()
            S.op("pool", lambda e, src=src, sv=sv: e.dma_start(out=sv, in_=src), writes=stg_rs[sl], dma=True)

            def out_dma(dst=dst, sv=sv, sl=sl, q=q):
                S.op("pool", lambda e: e.dma_start(out=dst, in_=sv), reads=stg_rs[sl], writes=[r_wconv[q]], dma=True)
            cstate["pend"] = out_dma

    def finish_conv():
        if cstate["done"]:
            return
        pump(32)
        if cstate.get("pend") is not None:
            cstate["pend"]()
            cstate["pend"] = None
        cstate["done"] = True
        for (vt, rs, nm) in ((Vr4, R("stgA"), "Vr4"), (Vnat, R("stgB"), "Vnat")):
            vv = vt[:, 4:8, :].rearrange("p t (h c) -> p t h c", h=4)
            S.op("pool", lambda e, vv=vv: e.memset(vv[:, :, :, 64:128], 1.0), writes=[rs, R(nm)])

    S.op("sp", lambda e: e.dma_start(out=gpre[:, :, :], in_=gpre_d.rearrange("p (a c) -> p a c", a=2)),
         writes=[R("gpre")], dma=True)
    S.op("sp", lambda e: e.dma_start(out=gpost[:, :, :], in_=gpost_d.rearrange("p (a c) -> p a c", a=2)),
         writes=[R("gpost")], dma=True)
    S.op("sp", lambda e: e.dma_start(out=gmix[:, :, :], in_=gmix_d.rearrange("p (a c) -> p a c", a=2)),
         writes=[R("gmix")], dma=True)
    S.op("sp", lambda e: e.dma_start(out=convw[:, :, :], in_=convw_d.rearrange("p (a c) -> p a c", a=4)),
         writes=[R("convw")], dma=True)
    S.op("sp", lambda e: e.dma_start(out=ident_f[:, :], in_=ident_d), writes=[R("ident_f")], dma=True)
    S.op("pool", lambda e: e.memset(ones_bf[:, :], 1.0), writes=[R("ones_bf")])
    S.op("pool", lambda e: e.memset(epsc[:, :], EPS), writes=[R("epsc")])
    for nm, vt, n in (("Vnat", Vnat, 8), ("Vr4", Vr4, 8), ("Vr16", Vr16, 16)):
        vv = vt[:, :, :].rearrange("p t (h c) -> p t h c", h=4)
        S.op("pool", lambda e, vv=vv: e.memset(vv[:, :, :, 64:128], 1.0), writes=[R(nm)])

    def rstd_ops(ss_ap, out_ap, n, rn):
        S.op("act", lambda e: e.activation(out=out_ap, in_=ss_ap, func=AF.Ln, scale=1.0 / n, bias=epsc[0:ss_ap.shape[0], :]),
             reads=[rn, R("epsc")], writes=[rn])
        S.op("act", lambda e: e.activation(out=out_ap, in_=out_ap, func=AF.Exp, scale=-0.5), reads=[rn], writes=[rn])

    def phase_A(b, g):
        S.op("sp", lambda e: e.dma_start(out=csg[:, :, :],
                                         in_=csp_d.rearrange("p (t c) -> p t c", t=16)[:, 4 * g:4 * g + 4, :]),
             writes=[R("csg")], dma=True)

        def head(t):
            i = 4 * g + t
            tok0 = 128 * i
            xt = xring[t % 2]
            rx = R(f"s1_{t % 2}")
            qb = (1, 2, 3) if t % 2 == 0 else (5, 6, 7)
            S.op("sp", lambda e: e.dma_start(out=xt, in_=xp[b, tok0:tok0 + 128, :]), writes=[rx], dma=True)
            S.op("act", lambda e: e.activation(out=junk[:, :], in_=xt, func=AF.Square, accum_out=small[:, 0:1]),
                 reads=[rx], writes=[R("junk"), R("ss0")])
            rstd_ops(small[:, 0:1], small[:, 1:2], 1024.0, R("ss0"))
            S.op("dve", lambda e: e.tensor_scalar(out=hbf, in0=xt, scalar1=small[:, 1:2], scalar2=None,
                                                  op0=ALU.mult), reads=[rx, R("ss0")], writes=[*r_hbf])
            psT = bank(0).bitcast(BF16)
            for c in range(8):
                S.op("pe", lambda e, c=c: e.transpose(out=psT[:, c * 128:(c + 1) * 128],
                                                      in_=hbf[:, c * 128:(c + 1) * 128], identity=ident_bf[:, :]),
                     reads=[*r_hbf, R("ident_bf")], writes=[RB(0)])
            S.op("dve", lambda e: e.tensor_tensor(
                out=hT[:, :, t * 128:(t + 1) * 128], in0=psT.rearrange("p (c n) -> p c n", c=8),
                in1=bcast_last(gpre[:, 0, :], 128), op=ALU.mult),
                reads=[RB(0), R("gpre")], writes=[R(f"hT{t}"), *r_hT])
            for n in range(3):
                for c in range(8):
                    S.op("pe", lambda e, n=n, c=c: e.matmul(
                        bank(qb[n]), lhsT=hT[:, c, t * 128:(t + 1) * 128],
                        rhs=w_in_sb[:, c, n * 512:(n + 1) * 512], start=(c == 0), stop=(c == 7)),
                        reads=[R(f"hT{t}"), *r_hT, R(f"w_in{n}")], writes=[RB(qb[n])])

        def tail(t):
            i = 4 * g + t
            tok0 = 128 * i
            qb = (1, 2, 3) if t % 2 == 0 else (5, 6, 7)
            cosb = bcast_mid(csg[:, t, 0:32], 8)
            sinb = bcast_mid(csg[:, t, 32:64], 8)
            tA = ropeA.rearrange("p (h d) -> p h d", h=8)
            tB = ropeB.rearrange("p (h d) -> p h d", h=8)
            tC = cbuf[:, 512:768].rearrange("p (h d) -> p h d", h=8)
            tD = cbuf[:, 768:1024].rearrange("p (h d) -> p h d", h=8)
            r_tC, r_tD = [R("cbufb")], [R("cbufb2")]

            def rope(src_bank, dst, rdst, fin):
                src = bank(src_bank).rearrange("p (h two d) -> p h two d", h=8, two=2)
                dv = dst.rearrange("p (h two d) -> p h two d", h=8, two=2)
                rsrc = RB(src_bank)
                S.op("dve", lambda e: e.tensor_tensor(out=tA, in0=src[:, :, 0, :], in1=cosb, op=ALU.mult),
                     reads=[rsrc, R("csg")], writes=[*r_ropeA])
                S.op("dve", lambda e: e.tensor_tensor(out=tB, in0=src[:, :, 1, :], in1=sinb, op=ALU.mult),
                     reads=[rsrc, R("csg")], writes=[*r_ropeB])
                S.op("dve", lambda e: e.tensor_tensor(out=tC, in0=src[:, :, 1, :], in1=cosb, op=ALU.mult),
                     reads=[rsrc, R("csg")], writes=r_tC)
                S.op("dve", lambda e: e.tensor_tensor(out=tD, in0=src[:, :, 0, :], in1=sinb, op=ALU.mult),
                     reads=[rsrc, R("csg")], writes=r_tD)
                S.op(fin, lambda e: e.tensor_tensor(out=dv[:, :, 0, :], in0=tA, in1=tB, op=ALU.subtract),
                     reads=[*r_ropeA, *r_ropeB], writes=rdst)
                S.op(fin, lambda e: e.tensor_tensor(out=dv[:, :, 1, :], in0=tC, in1=tD, op=ALU.add),
                     reads=[*r_tC, *r_tD], writes=rdst)

            S.op("act", lambda e: e.activation(out=v_f, in_=bank(qb[2]), func=AF.Copy), reads=[RB(qb[2])],
                 writes=[R("s5b")])
            rope(qb[0], q_r, r_qr, "dve")
            rope(qb[1], k_r, r_kr, "pool")
            S.op("act", lambda e: e.activation(out=k_rb, in_=k_r, func=AF.Copy), reads=[R("s5a")], writes=[*r_krb])
            slot = i % 8
            vdst = Vnat[:, slot, :].rearrange("p (h c) -> p h c", h=4)
            vsrc = v_f.rearrange("p (h a d) -> p h a d", h=4, a=2)
            S.op("pool", lambda e: e.tensor_copy(out=vdst[:, :, 0:64], in_=vsrc[:, :, 0, :]),
                 reads=[R("s5b")], writes=[R("Vnat")])
            S.op("pool", lambda e: e.tensor_copy(out=vdst[:, :, 128:192], in_=vsrc[:, :, 1, :]),
                 reads=[R("s5b")], writes=[R("Vnat")])
            S.op("pool", lambda e: e.dma_start(out=kp[b, tok0:tok0 + 128, :], in_=k_r), reads=[R("s5a")], dma=True)
            S.op("pool", lambda e: e.dma_start(out=vp[b, tok0:tok0 + 128, :], in_=v_f), reads=[R("s5b")], dma=True)
            S.op("pool", lambda e: e.dma_start(out=vscr[b, tok0:tok0 + 128, :], in_=Vnat[:, slot, :]),
                 reads=[R("Vnat")], writes=[R("vscr")], dma=True)
            psQ = bank(4).bitcast(BF16)
            for c in range(4):
                S.op("pe", lambda e, c=c: e.transpose(out=psQ[:, c * 128:(c + 1) * 128],
                                                      in_=q_r[:, c * 128:(c + 1) * 128], identity=ident_bf[:, :]),
                     reads=[*r_qr, R("ident_bf")], writes=[RB(4)])
            for c in range(4):
                S.op("pe", lambda e, c=c: e.transpose(out=psQ[:, 512 + c * 128:512 + (c + 1) * 128],
                                                      in_=k_rb[:, c * 128:(c + 1) * 128], identity=ident_bf[:, :]),
                     reads=[*r_krb, R("ident_bf")], writes=[RB(4)])
            S.op("act", lambda e: e.activation(out=qT[:, :, t * 128:(t + 1) * 128],
                                               in_=psQ[:, 0:512].rearrange("p (c n) -> p c n", c=4),
                                               func=AF.Copy), reads=[RB(4)], writes=[R("qT")])
            S.op("act", lambda e: e.activation(out=kT[:, :, tok0:tok0 + 128],
                                               in_=psQ[:, 512:1024].rearrange("p (c n) -> p c n", c=4),
                                               func=AF.Copy), reads=[RB(4)], writes=[R("kT")])

        head(0)
        pump(1)
        for t in range(4):
            if t + 1 < 4:
                head(t + 1)
                pump(1)
            tail(t)
            pump(1)
        r_hTall = [R("hT0"), R("hT1"), R("hT2"), R("hT3")]
        if g == 0:
            S.op("pool", lambda e: e.memset(gi[:, :, 0:2], 0.0), writes=[*r_gi])
        else:
            S.op("pool", lambda e: e.tensor_copy(out=gi[:, :, 0:2], in_=carry[:, :, :]),
                 reads=[R("carry")], writes=[*r_gi])

        def conv_mm(j):
            bks = (1, 2, 3) if j % 2 == 0 else (5, 6, 7)
            for (bk, col0) in ((bks[0], 2048), (bks[1], 2560), (bks[2], 1536)):
                for c in range(8):
                    S.op("pe", lambda e, bk=bk, col0=col0, c=c: e.matmul(
                        bank(bk), lhsT=w_in_sb[:, c, col0 + j * 128:col0 + (j + 1) * 128],
                        rhs=hT[:, c, :], start=(c == 0), stop=(c == 7)),
                        reads=[*r_hTall, *r_hT, R(f"w_in{col0 // 512}")], writes=[RB(bk)])

        def conv_ew(j):
            bks = (1, 2, 3) if j % 2 == 0 else (5, 6, 7)
            ub = u_sb if j % 2 == 0 else cbuf[:, 0:512]
            tb = t0 if j % 2 == 0 else cbuf[:, 512:1024]
            rub = r_usb if j % 2 == 0 else [R("cbufa")]
            rtb = r_t0 if j % 2 == 0 else [R("cbufb"), R("cbufb2")]
            S.op("act", lambda e: e.activation(out=ub, in_=bank(bks[1]), func=AF.Copy), reads=[RB(bks[1])], writes=rub)
            S.op("dve", lambda e: e.tensor_tensor(out=gi[:, j, 2:514], in0=bank(bks[0]), in1=ub, op=ALU.mult),
                 reads=[RB(bks[0]), *rub], writes=[R(f"gi{j}")])
            S.op("act", lambda e: e.activation(out=tb, in_=gi[:, j, 0:512], func=AF.Copy, scale=convw[:, j, 0:1]),
                 reads=[R(f"gi{j}"), *r_gi, R("convw")], writes=rtb)
            for tap in (1, 2):
                S.op("dve", lambda e, tap=tap: e.scalar_tensor_tensor(
                    out=tb, in0=gi[:, j, tap:tap + 512], scalar=convw[:, j, tap:tap + 1], in1=tb,
                    op0=ALU.mult, op1=ALU.add), reads=[R(f"gi{j}"), *r_gi, R("convw"), *rtb], writes=rtb)
            S.op("dve", lambda e: e.tensor_tensor(out=convT[:, j, :], in0=bank(bks[2]), in1=tb, op=ALU.mult),
                 reads=[RB(bks[2]), *rtb], writes=[R(f"convT{j}")])
            S.op("act", lambda e: e.activation(out=sqA, in_=convT[:, j, :], func=AF.Square),
                 reads=[R(f"convT{j}")], writes=r_s9b)
            S.op("pe", lambda e: e.matmul(bank(4), lhsT=ones_bf[:, :], rhs=sqA, start=(j == 0), stop=(j == 3)),
                 reads=[*r_s9b, R("ones_bf")], writes=[RB(4)])

        conv_mm(0)
        for j in range(4):
            if j + 1 < 4:
                conv_mm(j + 1)
            conv_ew(j)
        r_cTall = [R("convT0"), R("convT1"), R("convT2"), R("convT3"), R("convT")]
        S.op("act", lambda e: e.activation(out=rbA, in_=bank(4), func=AF.Ln, scale=1.0 / 512, bias=epsc[:, :]),
             reads=[RB(4), R("epsc")], writes=r_s9b)
        S.op("act", lambda e: e.activation(out=rbA, in_=rbA, func=AF.Exp, scale=-0.5), reads=r_s9b, writes=r_s9b)
        for j in range(4):
            S.op("dve", lambda e, j=j: e.scalar_tensor_tensor(
                out=convT[:, j, :], in0=convT[:, j, :], scalar=gmix[:, 1, j:j + 1], in1=rbA,
                op0=ALU.mult, op1=ALU.mult), reads=[*r_cTall, *r_s9b, R("gmix")], writes=r_cTall)
        S.op("pool", lambda e: e.tensor_copy(out=carry[:, :, :], in_=gi[:, :, 512:514]),
             reads=[*r_gi, R("gi0"), R("gi1"), R("gi2"), R("gi3")], writes=[R("carry")])
        if g == 3:
            for j in range(4):
                S.op("pool", lambda e, j=j: e.dma_start(
                    out=cp[b, :, j * 128:(j + 1) * 128].rearrange("t p -> p t"),
                    in_=carry[:, j, :], allow_slow_non_contiguous=True),
                    reads=[R("carry")], dma=True)
        base4 = (g % 2) * 4
        src4 = vscr[b, 512 * g:512 * g + 512, :].rearrange("(m r) n -> m r n", r=4)
        S.op("sp", lambda e: e.dma_start(out=Vr4[:, base4:base4 + 4, :], in_=src4),
             reads=[R("vscr")], writes=[R("Vr4")], dma=True)
        src16 = vscr[b].rearrange("(m r) n -> m r n", r=16)[32 * g:32 * g + 32]
        S.op("sp", lambda e: e.dma_start(out=Vr16[32 * g:32 * g + 32, :, :], in_=src16),
             reads=[R("vscr")], writes=[R("Vr16")], dma=True)

    def phase_B(b, g):
        units = []
        for hp in range(4):
            for ab in range(2):
                rows = slice(64 * ab, 64 * ab + 64)
                vcol = hp * 192 + 64 * ab
                for half in range(2):
                    blocks = []
                    for qb in range(2):
                        i = 4 * g + 2 * half + qb
                        qc = (2 * half + qb) * 128
                        qap = qT[rows, hp, qc:qc + 128]
                        prev = None if i == 0 else (kT[rows, hp, 128 * (i - 1):128 * i],
                                                    Vnat[:, (i - 1) % 8, vcol:vcol + 128])
                        diag = (kT[rows, hp, 128 * i:128 * i + 128], Vnat[:, i % 8, vcol:vcol + 128])
                        blocks.append((qap, prev, diag, qc))
                    variant = 1 if (g == 0 and half == 0) else 0
                    units.append(dict(kind=12, hp=hp, pair_last=False, ab=ab, blocks=blocks, variant=variant, otile=0,
                                      first=(half == 0), last=(half == 1), vres="Vnat"))
                for half in range(2):
                    blocks = []
                    for rr in range(2):
                        r = 2 * half + rr
                        qap = qT[rows, hp, r:512:4]
                        prev = None if g == 0 else (kT[rows, hp, 512 * (g - 1) + r:512 * g:4],
                                                    Vr4[:, ((g - 1) % 2) * 4 + r, vcol:vcol + 128])
                        diag = (kT[rows, hp, 512 * g + r:512 * (g + 1):4], Vr4[:, (g % 2) * 4 + r, vcol:vcol + 128])
                        blocks.append((qap, prev, diag, r * 128))
                    variant = 2 if g == 0 else 0
                    units.append(dict(kind=12, hp=hp, pair_last=False, ab=ab, blocks=blocks, variant=variant, otile=1,
                                      first=(half == 0), last=(half == 1), vres="Vr4"))
                units.append(dict(kind=3, hp=hp, pair_last=(ab == 1), ab=ab, otile=2, first=True, last=True, rows=rows, vcol=vcol))

        def s_fn(k):
            u = units[k]
            hp = u["hp"]
            psS = bank(SBK[k % 4])
            rS = RB(SBK[k % 4])
            if u["kind"] == 12:
                S.op("pe", lambda e: e.matmul(psS, lhsT=ident_bf[:, :], rhs=mask12[:, u["variant"], :],
                                              start=True, stop=False),
                     reads=[R("ident_bf"), R("mask12")], writes=[rS])
                nb = len(u["blocks"])
                for bi, (qap, prev, diag, oc) in enumerate(u["blocks"]):
                    if prev is not None:
                        S.op("pe", lambda e, bi=bi, qap=qap, prev=prev: e.matmul(
                            psS[:, bi * 256:bi * 256 + 128], lhsT=prev[0], rhs=qap, start=False, stop=False),
                            reads=[R("kT"), R("qT")], writes=[rS])
                    S.op("pe", lambda e, bi=bi, qap=qap, diag=diag: e.matmul(
                        psS[:, bi * 256 + 128:bi * 256 + 256], lhsT=diag[0], rhs=qap, start=False,
                        stop=(bi == nb - 1)), reads=[R("kT"), R("qT")], writes=[rS])
            else:
                M = 32 * (g + 1)
                rows = u["rows"]
                S.op("pe", lambda e: e.matmul(psS[0:M, :], lhsT=ident_bf[:, 0:M], rhs=mask3[:, g, :],
                                              start=True, stop=False),
                     reads=[R("ident_bf"), R("mask3")], writes=[rS])
                for r16 in range(16):
                    S.op("pe", lambda e, r16=r16: e.matmul(
                        psS[0:M, r16 * 32:r16 * 32 + 32], lhsT=kT[rows, hp, r16:r16 + 16 * (M - 1) + 1:16],
                        rhs=qT[rows, hp, r16:512:16], start=False, stop=(r16 == 15)),
                        reads=[R("kT"), R("qT")], writes=[rS])

        def e_fn(k):
            u = units[k]
            psS = bank(SBK[k % 4])
            P = Pb4[k % 4]
            M = 128 if u["kind"] == 12 else 32 * (g + 1)
            S.op("act", lambda e: e.activation(out=P[0:M, :], in_=psS[0:M, :], func=AF.Exp, scale=0.125),
                 reads=[RB(SBK[k % 4])], writes=r_P4[k % 4])

        def pv_fn(k):
            u = units[k]
            hp = u["hp"]
            P = Pb4[k % 4]
            rP = r_P4[k % 4]
            ob = 2 + (u["otile"] + u["ab"]) % 2
            psO = bank(ob)
            rO = RB(ob)
            acc = accs[u["ab"]]
            racc = r_acc[u['ab']]
            if u["kind"] == 12:
                for bi, (qap, prev, diag, oc) in enumerate(u["blocks"]):
                    if prev is not None:
                        S.op("pe", lambda e, bi=bi, prev=prev, oc=oc: e.matmul(
                            psO[:, oc:oc + 128], lhsT=prev[1], rhs=P[:, bi * 256:bi * 256 + 128],
                            start=True, stop=False), reads=[*rP, R(u["vres"])], writes=[rO])
                    S.op("pe", lambda e, bi=bi, diag=diag, oc=oc, prev=prev: e.matmul(
                        psO[:, oc:oc + 128], lhsT=diag[1], rhs=P[:, bi * 256 + 128:bi * 256 + 256],
                        start=(prev is None), stop=True), reads=[*rP, R(u["vres"])], writes=[rO])
                if u["last"]:
                    if u["otile"] == 0:
                        S.op("dve", lambda e: e.tensor_copy(out=acc, in_=psO), reads=[rO], writes=racc)
                    else:
                        av = acc.rearrange("p (m r) -> p r m", r=4)
                        S.op("dve", lambda e: e.tensor_tensor(out=av, in0=psO.rearrange("p (r m) -> p r m", r=4),
                                                              in1=av, op=ALU.add),
                             reads=[rO, *racc], writes=racc)
            else:
                M = 32 * (g + 1)
                vcol = u["vcol"]
                for r16 in range(16):
                    S.op("pe", lambda e, r16=r16: e.matmul(
                        psO[:, r16 * 32:r16 * 32 + 32], lhsT=Vr16[0:M, r16, vcol:vcol + 128],
                        rhs=P[0:M, r16 * 32:r16 * 32 + 32], start=True, stop=True),
                        reads=[*rP, R("Vr16")], writes=[rO])
                av = acc.rearrange("p (j r) -> p r j", r=16)
                S.op("dve", lambda e: e.tensor_tensor(out=av, in0=psO.rearrange("p (r j) -> p r j", r=16),
                                                      in1=av, op=ALU.add), reads=[rO, *racc], writes=racc)

        def finish_pair(hp):
            S.op("dve", lambda e: e.reciprocal(out=Rb[0:64, :], in_=accs[0][64:128, :]),
                 reads=[*r_acc[0]], writes=[*r_kr])
            S.op("dve", lambda e: e.reciprocal(out=Rb[64:128, :], in_=accs[1][0:64, :]),
                 reads=[*r_acc[1]], writes=[*r_kr])
            S.op("dve", lambda e: e.tensor_tensor(out=attnT[0:64, hp, :], in0=accs[0][0:64, :],
                                                  in1=Rb[0:64, :], op=ALU.mult),
                 reads=[*r_acc[0], *r_kr], writes=[R(f"attnT{hp}"), R("attnT")])
            S.op("dve", lambda e: e.tensor_tensor(out=attnT[64:128, hp, :], in0=accs[1][64:128, :],
                                                  in1=Rb[64:128, :], op=ALU.mult),
                 reads=[*r_acc[1], *r_kr], writes=[R(f"attnT{hp}")])
            sq = sqB if hp % 2 == 0 else sqB2
            rsq = r_sqB if hp % 2 == 0 else [R("cbufa")]
            S.op("pool", lambda e: e.tensor_tensor(out=sq, in0=attnT[:, hp, :], in1=attnT[:, hp, :], op=ALU.mult),
                 reads=[R(f"attnT{hp}")], writes=rsq)

            def later():
                S.op("pe", lambda e: e.matmul(bank(4), lhsT=ones_bf[:, :], rhs=sq, start=(hp == 0), stop=(hp == 3)),
                     reads=[*rsq, R("ones_bf")], writes=[RB(4)])
            return later

        if dbg is not None and dbg.get("skip"):
            units[:] = [u for u in units if (u["otile"] not in dbg["skip"])]
        n = len(units)
        pending = []
        for k0 in range(min(3, n)):
            s_fn(k0)
        for k in range(n):
            if k + 3 < n:
                s_fn(k + 3)
            e_fn(k)
            pv_fn(k)
            pump(1)
            for item in list(pending):
                item[0] -= 1
                if item[0] <= 0:
                    item[1]()
                    pending.remove(item)
            if units[k]["pair_last"]:
                pending.append([3, finish_pair(units[k]["hp"])])
        for item in pending:
            item[1]()
        S.op("act", lambda e: e.activation(out=rbB, in_=bank(4), func=AF.Ln, scale=1.0 / 512, bias=epsc[:, :]),
             reads=[RB(4), R("epsc")], writes=[*r_vf])
        S.op("act", lambda e: e.activation(out=rbB, in_=rbB, func=AF.Exp, scale=-0.5), reads=[*r_vf], writes=[*r_vf])
        for hp in range(4):
            S.op("dve", lambda e, hp=hp: e.scalar_tensor_tensor(
                out=attnT[:, hp, :], in0=attnT[:, hp, :], scalar=gmix[:, 0, hp:hp + 1], in1=rbB,
                op0=ALU.mult, op1=ALU.mult), reads=[R(f"attnT{hp}"), *r_vf, R("gmix")], writes=[R(f"attnT{hp}"), R("attnT")])

    def norm_token_major(src_ap, rsrc, ntok, col):
        S.op("act", lambda e: e.activation(out=junk[0:ntok, :], in_=src_ap, func=AF.Square,
                                           accum_out=small[0:ntok, col:col + 1]),
             reads=rsrc, writes=[R("junk"), R(f"ss{col}")])
        rstd_ops(small[0:ntok, col:col + 1], small[0:ntok, col + 1:col + 2], 1024.0, R(f"ss{col}"))

    def mlp_block(ntiles, ntok, x_src_fn, y_dst_fn, cat_fn, tokw):
        def wload(f):
            sl = (f // 2) % 2
            S.op("sp", lambda e: e.dma_start(
                out=wu_ring[sl], in_=wup_s.rearrange("(c p) n -> p c n", p=128)[:, :, f * 128:f * 128 + 256]),
                reads=r_wconv[0:16], writes=[R(f"wu{sl}")], dma=True)
            S.op("sp", lambda e: e.dma_start(
                out=wd_ring[sl], in_=wdn_s[f * 128:f * 128 + 256, :].rearrange("(k p) n -> p k n", p=128)),
                reads=r_wconv[16:32], writes=[R(f"wd{sl}")], dma=True)

        def head_mm():
            wload(0)
            wload(2)
            for cs_ in ((4, 5, 6, 7), (0, 1, 2, 3)):
                for t in range(ntiles):
                    mixed = pp[2 + t]
                    for half in range(2):
                        for c in cs_:
                            S.op("pe", lambda e, half=half, c=c, t=t, mixed=mixed: e.matmul(
                                mixed[0:ntok, half * 512:(half + 1) * 512], lhsT=cat_fn(c, t),
                                rhs=w_out_sb[:, c, half * 512:(half + 1) * 512], start=(c == 4), stop=(c == 3)),
                                reads=[*(r_attnAll if c < 4 else r_convAll), R("w_out")],
                                writes=[RB(4 + 2 * t + half)])

        qT_f = qT[:, :, :].rearrange("p c n -> p (c n)").bitcast(F32)
        tb = [dict(tmp=tmpC, rtmp=r_tmpC, xr=xr, rxr=[R("s5a"), R("s5b")], h2=h2, rh2=r_h2, cols=(2, 4, 6)),
              dict(tmp=cbuf[:, :], rtmp=[R("cbufa"), R("cbufb"), R("cbufb2")], xr=qT_f, rxr=[R("qT")],
                   h2=s6[:, 512:1024].bitcast(BF16), rh2=ee[4:8], cols=(8, 10, 12))]

        def head_ew():
            for t in range(ntiles):
                B_ = tb[t]
                mixed = pp[2 + t]
                rmix = [RB(4 + 2 * t), RB(5 + 2 * t)]
                c1, c2 = B_["cols"][0], B_["cols"][1]
                S.op("sp", lambda e, t=t, B_=B_: e.dma_start(out=B_["xr"][0:ntok, :], in_=x_src_fn(t)), writes=B_["rxr"],
                     dma=True)
                norm_token_major(mixed[0:ntok, :], rmix, ntok, c1)
                S.op("dve", lambda e, mixed=mixed, B_=B_, c1=c1: e.scalar_tensor_tensor(
                    out=B_["tmp"][0:ntok, :], in0=mixed[0:ntok, :], scalar=small[0:ntok, c1 + 1:c1 + 2],
                    in1=gpost[0:ntok, 0, :], op0=ALU.mult, op1=ALU.mult),
                    reads=[*rmix, R(f"ss{c1}"), R("gpost")], writes=B_["rtmp"])
                x1 = x1buf[t]
                rx1 = R(f"s1_{t}")
                S.op("pool", lambda e, x1=x1, B_=B_: e.tensor_tensor(out=x1[0:ntok, :], in0=B_["tmp"][0:ntok, :],
                                                                     in1=B_["xr"][0:ntok, :], op=ALU.add),
                     reads=[*B_["rtmp"], *B_["rxr"]], writes=[rx1])
                norm_token_major(x1[0:ntok, :], [rx1], ntok, c2)
                S.op("dve", lambda e, x1=x1, B_=B_, c2=c2: e.tensor_scalar(
                    out=B_["h2"][0:ntok, :], in0=x1[0:ntok, :], scalar1=small[0:ntok, c2 + 1:c2 + 2], scalar2=None,
                    op0=ALU.mult), reads=[rx1, R(f"ss{c2}")], writes=B_["rh2"])
                tb_ = 0 if t == 0 else 2
                psT = bank(tb_).bitcast(BF16)
                for c in range(8):
                    S.op("pe", lambda e, c=c, psT=psT, B_=B_: e.transpose(out=psT[:, c * ntok:(c + 1) * ntok],
                                                                          in_=B_["h2"][0:ntok, c * 128:(c + 1) * 128],
                                                                          identity=ident_bf[0:ntok, 0:ntok]),
                         reads=[*B_["rh2"], R("ident_bf")], writes=[RB(tb_)])
                S.op("dve", lambda e, t=t, psT=psT: e.tensor_tensor(
                    out=h2T[:, :, t * ntok:(t + 1) * ntok], in0=psT[:, 0:8 * ntok].rearrange("p (c n) -> p c n", c=8),
                    in1=bcast_last(gpre[:, 1, :], ntok), op=ALU.mult),
                    reads=[RB(tb_), R("gpre")], writes=r_h2T)
        W = ntiles * ntok

        def up(f):
            sl = (f // 2) % 2
            psU = bank(6 + f % 2)
            for c in range(8):
                S.op("pe", lambda e, c=c: e.matmul(psU[:, 0:W], lhsT=wu_ring[sl][:, c, (f % 2) * 128:(f % 2) * 128 + 128],
                                                   rhs=h2T[:, c, 0:W], start=(c == 0), stop=(c == 7)),
                     reads=[R(f"wu{sl}"), *r_h2T], writes=[RB(6 + f % 2)])
            S.op("act", lambda e: e.activation(out=r_ring[f % 2][:, 0:W], in_=psU[:, 0:W], func=AF.Relu),
                 reads=[RB(6 + f % 2)], writes=r_r[f % 2])
            S.op("pool", lambda e: e.tensor_tensor(out=r2_ring[f % 2][:, 0:W], in0=r_ring[f % 2][:, 0:W],
                                                   in1=r_ring[f % 2][:, 0:W], op=ALU.mult),
                 reads=r_r[f % 2], writes=r_r2[f % 2])

        def down(f):
            sl = (f // 2) % 2
            for t in range(ntiles):
                for half in range(2):
                    S.op("pe", lambda e, t=t, half=half: e.matmul(
                        pp[t][0:ntok, half * 512:(half + 1) * 512], lhsT=r2_ring[f % 2][:, t * ntok:(t + 1) * ntok],
                        rhs=wd_ring[sl][:, f % 2, half * 512:(half + 1) * 512], start=(f == 0), stop=(f == 31)),
                        reads=[*r_r2[f % 2], R(f"wd{sl}")], writes=[RB(2 * t + half)])

        def loop():
            up(0)
            for f in range(32):
                if f + 1 < 32:
                    up(f + 1)
                down(f)
                if f % 2 == 1 and f + 3 < 32:
                    wload(f + 3)
        def tail():
            for t in range(ntiles):
                B_ = tb[t]
                c3 = B_["cols"][2]
                fp = pp[t]
                norm_token_major(fp[0:ntok, :], [RB(2 * t), RB(2 * t + 1)], ntok, c3)
                S.op("dve", lambda e, fp=fp, B_=B_, c3=c3: e.scalar_tensor_tensor(
                    out=B_["tmp"][0:ntok, :], in0=fp[0:ntok, :], scalar=small[0:ntok, c3 + 1:c3 + 2],
                    in1=gpost[0:ntok, 1, :], op0=ALU.mult, op1=ALU.mult),
                    reads=[RB(2 * t), RB(2 * t + 1), R(f"ss{c3}"), R("gpost")], writes=B_["rtmp"])
                x1 = x1buf[t]
                rx1 = R(f"s1_{t}")
                S.op("pool", lambda e, x1=x1, B_=B_: e.tensor_tensor(out=B_["xr"][0:ntok, :], in0=B_["tmp"][0:ntok, :],
                                                                     in1=x1[0:ntok, :], op=ALU.add),
                     reads=[*B_["rtmp"], rx1], writes=B_["rxr"])
                S.op("pool", lambda e, t=t, B_=B_: e.dma_start(out=y_dst_fn(t), in_=B_["xr"][0:ntok, :]),
                     reads=B_["rxr"], dma=True)
        return head_mm, head_ew, loop, tail

    def phase_C(b, g):
        parts = []
        for sub in range(2):
            def x_src(t, sub=sub):
                i = 4 * g + 2 * sub + t
                return xp[b, 128 * i:128 * i + 128, :]

            def y_dst(t, sub=sub):
                i = 4 * g + 2 * sub + t
                return yp[b, 128 * i:128 * i + 128, :]

            def cat(c, t, sub=sub):
                tc = (2 * sub + t) * 128
                return (attnT if c < 4 else convT)[:, c % 4, tc:tc + 128]

            parts.append(mlp_block(2, 128, x_src, y_dst, cat, 256))
        p0, p1 = parts
        p0[0](); p0[1](); p0[2]()
        p1[0]()
        p0[3]()
        p1[1](); p1[2](); p1[3]()

    def sample_all():
        NT = 64
        S.op("pool", lambda e: e.dma_start(out=smask8[:, :, :], in_=smask_d.rearrange("p (h t) -> p h t", h=8)),
             writes=[R("smask8")], dma=True)
        S.op("pool", lambda e: e.dma_start(out=wnew8[:, :, :], in_=wnew_d.rearrange("p (h t) -> p h t", h=8)),
             writes=[R("wnew8")], dma=True)
        S.op("sp", lambda e: e.dma_start(out=csg[0:NT, 0, :], in_=css_d), writes=[R("csg")], dma=True)
        xt = xring[0]
        rx = R("s1_0")
        S.op("sp", lambda e: e.dma_start(out=xt[0:NT, :], in_=xs), writes=[rx], dma=True)
        S.op("act", lambda e: e.activation(out=junk[0:NT, :], in_=xt[0:NT, :], func=AF.Square,
                                           accum_out=small[0:NT, 0:1]), reads=[rx], writes=[R("junk"), R("ss0")])
        rstd_ops(small[0:NT, 0:1], small[0:NT, 1:2], 1024.0, R("ss0"))
        S.op("dve", lambda e: e.tensor_scalar(out=hbf[0:NT, :], in0=xt[0:NT, :], scalar1=small[0:NT, 1:2],
                                              scalar2=None, op0=ALU.mult), reads=[rx, R("ss0")], writes=[*r_hbf])
        psT = bank(0).bitcast(BF16)
        for c in range(8):
            S.op("pe", lambda e, c=c: e.transpose(out=psT[:, c * NT:(c + 1) * NT],
                                                  in_=hbf[0:NT, c * 128:(c + 1) * 128], identity=ident_bf[0:NT, 0:NT]),
                 reads=[*r_hbf, R("ident_bf")], writes=[RB(0)])
        S.op("dve", lambda e: e.tensor_tensor(out=hT[:, :, 0:NT], in0=psT[:, 0:8 * NT].rearrange("p (c n) -> p c n", c=8),
                                              in1=bcast_last(gpre[:, 0, :], NT), op=ALU.mult),
             reads=[RB(0), R("gpre")], writes=[*r_hT])
        for n in range(3):
            for c in range(8):
                S.op("pe", lambda e, n=n, c=c: e.matmul(bank(1 + n)[0:NT, :], lhsT=hT[:, c, 0:NT],
                                                        rhs=w_in_sb[:, c, n * 512:(n + 1) * 512],
                                                        start=(c == 0), stop=(c == 7)),
                     reads=[*r_hT, R(f"w_in{n}")], writes=[RB(1 + n)])
        cosb = bcast_last(csg[0:NT, 0, 0:32], 8)
        sinb = bcast_last(csg[0:NT, 0, 32:64], 8)
        tA = ropeA[0:NT, :].rearrange("p (d h) -> p d h", h=8)
        tB = ropeB[0:NT, :].rearrange("p (d h) -> p d h", h=8)

        def rope(src_bank, dst, rdst):
            src = bank(src_bank)[0:NT, :].rearrange("p (h two d) -> p two d h", h=8, two=2)
            dv = dst[0:NT, :].rearrange("p (h two d) -> p two d h", h=8, two=2)
            rsrc = RB(src_bank)
            S.op("dve", lambda e: e.tensor_tensor(out=tA, in0=src[:, 0], in1=cosb, op=ALU.mult),
                 reads=[rsrc, R("csg")], writes=[*r_ropeA])
            S.op("dve", lambda e: e.tensor_tensor(out=tB, in0=src[:, 1], in1=sinb, op=ALU.mult),
                 reads=[rsrc, R("csg")], writes=[*r_ropeB])
            S.op("pool", lambda e: e.tensor_tensor(out=dv[:, 0], in0=tA, in1=tB, op=ALU.subtract),
                 reads=[*r_ropeA, *r_ropeB], writes=rdst)
            S.op("dve", lambda e: e.tensor_tensor(out=tA, in0=src[:, 1], in1=cosb, op=ALU.mult),
                 reads=[rsrc, R("csg")], writes=[*r_ropeA])
            S.op("dve", lambda e: e.tensor_tensor(out=tB, in0=src[:, 0], in1=sinb, op=ALU.mult),
                 reads=[rsrc, R("csg")], writes=[*r_ropeB])
            S.op("pool", lambda e: e.tensor_tensor(out=dv[:, 1], in0=tA, in1=tB, op=ALU.add),
                 reads=[*r_ropeA, *r_ropeB], writes=rdst)

        rope(1, q_r, r_qr)
        rope(2, k_r, r_kr)
        S.op("act", lambda e: e.activation(out=k_rb[0:NT, :], in_=k_r[0:NT, :], func=AF.Copy),
             reads=[R("s5a")], writes=[*r_krb])
        S.op("act", lambda e: e.activation(out=v_f[0:NT, :], in_=bank(3)[0:NT, :], func=AF.Copy),
             reads=[RB(3)], writes=[R("s5b")])
        vdst = Vnat[0:NT, 0, :].rearrange("p (h c) -> p h c", h=4)
        vsrc = v_f[0:NT, :].rearrange("p (h a d) -> p h a d", h=4, a=2)
        S.op("pool", lambda e: e.tensor_copy(out=vdst[:, :, 0:64], in_=vsrc[:, :, 0, :]),
             reads=[R("s5b")], writes=[R("Vnat")])
        S.op("pool", lambda e: e.tensor_copy(out=vdst[:, :, 128:192], in_=vsrc[:, :, 1, :]),
             reads=[R("s5b")], writes=[R("Vnat")])
        S.op("pool", lambda e: e.dma_start(out=ks, in_=k_r[0:NT, :]), reads=[R("s5a")], dma=True)
        S.op("pool", lambda e: e.dma_start(out=vs, in_=v_f[0:NT, :]), reads=[R("s5b")], dma=True)
        psQ = bank(4).bitcast(BF16)
        for c in range(4):
            S.op("pe", lambda e, c=c: e.transpose(out=psQ[:, c * NT:(c + 1) * NT], in_=q_r[0:NT, c * 128:(c + 1) * 128],
                                                  identity=ident_bf[0:NT, 0:NT]),
                 reads=[*r_qr, R("ident_bf")], writes=[RB(4)])
        for c in range(4):
            S.op("pe", lambda e, c=c: e.transpose(out=psQ[:, 512 + c * NT:512 + (c + 1) * NT],
                                                  in_=k_rb[0:NT, c * 128:(c + 1) * 128], identity=ident_bf[0:NT, 0:NT]),
                 reads=[*r_krb, R("ident_bf")], writes=[RB(4)])
        S.op("act", lambda e: e.activation(out=qT[:, :, 0:NT], in_=psQ[:, 0:4 * NT].rearrange("p (c n) -> p c n", c=4),
                                           func=AF.Copy), reads=[RB(4)], writes=[R("qT")])
        S.op("act", lambda e: e.activation(out=qT[:, :, NT:2 * NT],
                                           in_=psQ[:, 512:512 + 4 * NT].rearrange("p (c n) -> p c n", c=4),
                                           func=AF.Copy), reads=[RB(4)], writes=[R("qT")])
        scs = junk[:, :].bitcast(F32)
        S.op("sp", lambda e: e.dma_start(out=scs[0:32, :], in_=scv.rearrange("s t n -> (s t) n")),
             writes=[R("junk")], dma=True)
        giS = gi[:, :, 0:96].rearrange("p j (s u) -> p j s u", u=6)
        for j in range(4):
            S.op("pe", lambda e, j=j: e.transpose(out=bank(0)[:, j * 32:(j + 1) * 32], in_=scs[0:32, j * 128:(j + 1) * 128],
                                                  identity=ident_f[0:32, 0:32]),
                 reads=[R("junk"), R("ident_f")], writes=[RB(0)])
        for j in range(4):
            S.op("dve", lambda e, j=j: e.tensor_copy(out=giS[:, j, :, 0:2],
                                                     in_=bank(0)[:, j * 32:(j + 1) * 32].rearrange("p (s t) -> p s t", t=2)),
                 reads=[RB(0)], writes=[*r_gi])
        for j in range(4):
            for (bk, col0) in ((5, 2048), (6, 2560), (7, 1536)):
                for c in range(8):
                    S.op("pe", lambda e, bk=bk, col0=col0, c=c, j=j: e.matmul(
                        bank(bk)[:, 0:NT], lhsT=w_in_sb[:, c, col0 + j * 128:col0 + (j + 1) * 128],
                        rhs=hT[:, c, 0:NT], start=(c == 0), stop=(c == 7)),
                        reads=[*r_hT, R(f"w_in{col0 // 512}")], writes=[RB(bk)])
            S.op("act", lambda e: e.activation(out=u_sb[:, 0:NT], in_=bank(6)[:, 0:NT], func=AF.Copy),
                 reads=[RB(6)], writes=[*r_usb])
            S.op("dve", lambda e, j=j: e.tensor_tensor(out=giS[:, j, :, 2:6],
                                                       in0=bank(5)[:, 0:NT].rearrange("p (s t) -> p s t", t=4),
                                                       in1=u_sb[:, 0:NT].rearrange("p (s t) -> p s t", t=4), op=ALU.mult),
                 reads=[RB(5), *r_usb], writes=[*r_gi])
            t0v = t0[:, 0:NT].rearrange("p (s t) -> p s t", t=4)
            S.op("pool", lambda e, j=j: e.tensor_scalar(out=t0v, in0=giS[:, j, :, 0:4], scalar1=convw[:, j, 0:1],
                                                        scalar2=None, op0=ALU.mult),
                 reads=[*r_gi, R("convw")], writes=[*r_t0])
            for tap in (1, 2):
                S.op("dve", lambda e, j=j, tap=tap: e.scalar_tensor_tensor(
                    out=t0v, in0=giS[:, j, :, tap:tap + 4], scalar=convw[:, j, tap:tap + 1], in1=t0v,
                    op0=ALU.mult, op1=ALU.add), reads=[*r_gi, R("convw"), *r_t0], writes=[*r_t0])
            S.op("dve", lambda e, j=j: e.tensor_tensor(out=convT[:, j, 0:NT], in0=bank(7)[:, 0:NT], in1=t0[:, 0:NT],
                                                       op=ALU.mult), reads=[RB(7), *r_t0], writes=[R("convT")])
            S.op("act", lambda e, j=j: e.activation(out=sqA[:, 0:NT], in_=convT[:, j, 0:NT], func=AF.Square),
                 reads=[R("convT")], writes=r_s9b)
            S.op("pe", lambda e, j=j: e.matmul(bank(4)[:, 0:NT], lhsT=ones_bf[:, :], rhs=sqA[:, 0:NT],
                                               start=(j == 0), stop=(j == 3)),
                 reads=[*r_s9b, R("ones_bf")], writes=[RB(4)])
        S.op("act", lambda e: e.activation(out=rbA[:, 0:NT], in_=bank(4)[:, 0:NT], func=AF.Ln, scale=1.0 / 512,
                                           bias=epsc[:, :]), reads=[RB(4), R("epsc")], writes=r_s9b)
        S.op("act", lambda e: e.activation(out=rbA[:, 0:NT], in_=rbA[:, 0:NT], func=AF.Exp, scale=-0.5),
             reads=r_s9b, writes=r_s9b)
        for j in range(4):
            S.op("dve", lambda e, j=j: e.scalar_tensor_tensor(
                out=convT[:, j, 0:NT], in0=convT[:, j, 0:NT], scalar=gmix[:, 1, j:j + 1], in1=rbA[:, 0:NT],
                op0=ALU.mult, op1=ALU.mult), reads=[R("convT"), *r_s9b, R("gmix")], writes=[R("convT")])
        nst = t0[:, 0:128].rearrange("p (j s t) -> p j s t", j=4, t=2)
        for j in range(4):
            S.op("pool", lambda e, j=j: e.tensor_copy(out=nst[:, j], in_=giS[:, j, :, 4:6]),
                 reads=[*r_gi], writes=[*r_t0])
        for j in range(4):
            S.op("pe", lambda e, j=j: e.transpose(out=bank(0)[0:32, j * 128:(j + 1) * 128],
                                                  in_=t0[:, j * 32:(j + 1) * 32], identity=ident_f[:, :]),
                 reads=[*r_t0, R("ident_f")], writes=[RB(0)])
        S.op("act", lambda e: e.activation(out=scs[0:32, :], in_=bank(0)[0:32, :], func=AF.Copy),
             reads=[RB(0)], writes=[R("junk")])
        S.op("pool", lambda e: e.dma_start(out=cso.rearrange("s t n -> (s t) n"), in_=scs[0:32, :]),
             reads=[R("junk")], dma=True)

        wflat = w_in_sb[:, :, :].rearrange("p c n -> p (c n)")
        sets = [
            dict(Kc=kT[:, :, :].rearrange("p c n -> p (c n)")[:, 0:9 * 512].rearrange("p (x n) -> p x n", x=9),
                 KTs=Vr4[:, :, :].rearrange("p t n -> p (t n)")[:, 0:4 * 1152].rearrange("p (c n) -> p c n", c=4),
                 Vc=Vr16[:, 0:9, :], rK=R("kT"), rKT=R("Vr4"), rVc=R("Vr16"), P=P_s, rP=R("P_s")),
            dict(Kc=Vnat[:, :, :].rearrange("p t n -> p (t n)")[:, 768:768 + 9 * 512].rearrange("p (x n) -> p x n", x=9),
                 KTs=wflat[:, 0:4608].rearrange("p (c n) -> p c n", c=4),
                 Vc=wflat[:, 4608:4608 + 9 * 768].rearrange("p (x n) -> p x n", x=9),
                 rK=R("Kc1"), rKT=R("KTs1"), rVc=R("Vc1"), P=P_s2, rP=R("P_s2")),
        ]
        Vstage = wflat[:, 11520:20736].bitcast(F32).rearrange("p (x n) -> p x n", x=9)
        vc1 = sets[1]["Vc"].rearrange("p x (h c) -> p x h c", h=4)
        S.op("pool", lambda e: e.memset(vc1[:, :, :, 64:128], 1.0),
             writes=[*[R(f"w_in{j}") for j in range(6)], R("Vnat"), R("Kc1"), R("KTs1"), R("Vc1"), R("Vstage")])
        for h in range(8):
            rows = slice(64 * (h % 2), 64 * (h % 2) + 64)
            bk = 5 + (h % 2)
            S.op("pe", lambda e, h=h, rows=rows, bk=bk: e.matmul(
                bank(bk)[0:NT, (h // 2) * NT:(h // 2 + 1) * NT], lhsT=qT[rows, h // 2, NT:2 * NT],
                rhs=qT[rows, h // 2, 0:NT], start=True, stop=True), reads=[R("qT")], writes=[RB(bk)])
        pnv = P_new[:, :, :].rearrange("p (c a) n -> p c a n", a=2)
        for a in range(2):
            S.op("act", lambda e, a=a: e.activation(out=pnv[:, :, a, :],
                                                    in_=bank(5 + a)[0:NT, 0:4 * NT].rearrange("p (c n) -> p c n", c=4),
                                                    func=AF.Exp, scale=0.125), reads=[RB(5 + a)], writes=[R("P_new")])
        S.op("dve", lambda e: e.tensor_tensor(out=P_new[:, :, :], in0=P_new[:, :, :], in1=wnew8[:, :, :], op=ALU.mult),
             reads=[R("P_new"), R("wnew8")], writes=[R("P_new")])
        psOs = bank(7)

        def s_loadK(sq):
            st = sets[sq % 2]
            dst, rr = st["Kc"], st["rK"]
            S.op("pool", lambda e: e.dma_start(out=dst[:, 0, :], in_=ck[sq, 1920:2048, :]), writes=[rr], dma=True)
            S.op("pool", lambda e: e.dma_start(out=dst[:, 1:5, :],
                                               in_=ck[sq, 1536:2048, :].rearrange("(i t) n -> i t n", t=4)),
                 writes=[rr], dma=True)
            S.op("pool", lambda e: e.dma_start(out=dst[:, 5:9, :],
                                               in_=ck[sq].rearrange("(i r) n -> i r n", r=16)[:, 0:4, :]),
                 writes=[rr], dma=True)

        def s_loadV(sq):
            S.op("sp", lambda e: e.dma_start(out=Vstage[:, 0, :], in_=cv[sq, 1920:2048, :]), writes=[R("Vstage")], dma=True)
            S.op("sp", lambda e: e.dma_start(out=Vstage[:, 1:5, :],
                                             in_=cv[sq, 1536:2048, :].rearrange("(i t) n -> i t n", t=4)),
                 writes=[R("Vstage")], dma=True)
            S.op("sp", lambda e: e.dma_start(out=Vstage[:, 5:9, :],
                                             in_=cv[sq].rearrange("(i r) n -> i r n", r=16)[:, 0:4, :]),
                 writes=[R("Vstage")], dma=True)

        def s_prepV(sq):
            st = sets[sq % 2]
            Kc, KTs, Vc = st["Kc"], st["KTs"], st["Vc"]
            vcv = Vc.rearrange("p x (h c) -> p x h c", h=4)
            vsv = Vstage.rearrange("p x (h a d) -> p x h a d", h=4, a=2)
            for x in range(9):
                for a_ in range(2):
                    eng = "act" if (x + a_) % 2 == 0 else "dve"
                    dcol = slice(0, 64) if a_ == 0 else slice(128, 192)
                    if eng == "act":
                        S.op("act", lambda e, x=x, a_=a_, dcol=dcol: e.activation(out=vcv[:, x, :, dcol], in_=vsv[:, x, :, a_, :],
                                                                                 func=AF.Copy),
                             reads=[R("Vstage")], writes=[st["rVc"]])
                    else:
                        S.op("dve", lambda e, x=x, a_=a_, dcol=dcol: e.tensor_copy(out=vcv[:, x, :, dcol], in_=vsv[:, x, :, a_, :]),
                             reads=[R("Vstage")], writes=[st["rVc"]])

        def s_prepK(sq):
            st = sets[sq % 2]
            Kc, KTs = st["Kc"], st["KTs"]
            for x in range(9):
                pb = bank(x % 2).bitcast(BF16)
                for c in range(4):
                    S.op("pe", lambda e, x=x, c=c, pb=pb: e.transpose(out=pb[:, c * 128:(c + 1) * 128],
                                                                      in_=Kc[:, x, c * 128:(c + 1) * 128],
                                                                      identity=ident_bf[:, :]),
                         reads=[st["rK"], R("ident_bf")], writes=[RB(x % 2)])
                if x % 2 == 0:
                    S.op("act", lambda e, x=x, pb=pb: e.activation(
                        out=KTs[:, :, x * 128:(x + 1) * 128], in_=pb[:, 0:512].rearrange("p (c n) -> p c n", c=4),
                        func=AF.Copy), reads=[RB(x % 2)], writes=[st["rKT"]])
                else:
                    S.op("dve", lambda e, x=x, pb=pb: e.tensor_copy(
                        out=KTs[:, :, x * 128:(x + 1) * 128], in_=pb[:, 0:512].rearrange("p (c n) -> p c n", c=4)),
                        reads=[RB(x % 2)], writes=[st["rKT"]])

        def s_comp(sq):
            st = sets[sq % 2]
            KTs, Vc, rP = st["KTs"], st["Vc"], st["rP"]
            Pq = st["P"][:, :, :].rearrange("p h n -> p (h n)")
            sc = bank(2)
            for c in range(4):
                S.op("pe", lambda e, c=c: e.matmul(sc[:, 0:32], lhsT=KTs[:, c, 0:128], rhs=Qbd[:, c, sq * 32:(sq + 1) * 32],
                                                   start=(c == 0), stop=(c == 3)),
                     reads=[st["rKT"], R("Qbd")], writes=[RB(2)])
            for (x0, co) in ((1, 32), (5, 64)):
                for t in range(4):
                    for c in range(4):
                        S.op("pe", lambda e, c=c, t=t, x0=x0, co=co: e.matmul(
                            sc[:, co + t * 8:co + t * 8 + 8], lhsT=KTs[:, c, (x0 + t) * 128:(x0 + t + 1) * 128],
                            rhs=Qbd[:, c, sq * 32 + t:sq * 32 + 32:4], start=(c == 0), stop=(c == 3)),
                            reads=[st["rKT"], R("Qbd")], writes=[RB(2)])
            S.op("act", lambda e: e.activation(out=Pq[:, 0:96], in_=sc[:, 0:96], func=AF.Exp, scale=0.125),
                 reads=[RB(2)], writes=[rP])
            S.op("dve", lambda e: e.tensor_tensor(out=Pq[:, 0:32], in0=Pq[:, 0:32],
                                                  in1=smask8[:, :, :].rearrange("p h t -> p (h t)"), op=ALU.mult),
                 reads=[rP, R("smask8")], writes=[rP])
            for h in range(8):
                vcol = (h // 2) * 192 + 64 * (h % 2)
                oc = sq * 32 + h * 4
                S.op("pe", lambda e, h=h, vcol=vcol, oc=oc: e.matmul(
                    psOs[:, oc:oc + 4], lhsT=Vc[:, 0, vcol:vcol + 128], rhs=Pq[:, h * 4:h * 4 + 4], start=True, stop=False),
                    reads=[st["rVc"], rP], writes=[RB(7)])
                for t in range(4):
                    for (x0, co) in ((1, 32), (5, 64)):
                        S.op("pe", lambda e, h=h, vcol=vcol, oc=oc, t=t, x0=x0, co=co: e.matmul(
                            psOs[:, oc + t:oc + t + 1], lhsT=Vc[:, x0 + t, vcol:vcol + 128],
                            rhs=Pq[:, co + t * 8 + h:co + t * 8 + h + 1], start=False, stop=False),
                            reads=[st["rVc"], rP], writes=[RB(7)])
                S.op("pe", lambda e, h=h, vcol=vcol, oc=oc: e.matmul(
                    psOs[:, oc:oc + 4], lhsT=Vnat[0:NT, 0, vcol:vcol + 128], rhs=P_new[:, h, 4 * sq:4 * sq + 4],
                    start=False, stop=True), reads=[R("Vnat"), R("P_new")], writes=[RB(7)])

        Qbd = Vr16[:, 9:12, :].rearrange("p t n -> p (t n)")[:, 0:2048].rearrange("p (c n) -> p c n", c=4)
        S.op("pool", lambda e: e.memset(Qbd, 0.0), writes=[R("Qbd"), R("Vr16")])
        for c in range(4):
            for a_ in range(2):
                rows = slice(64 * a_, 64 * a_ + 64)
                dst = Qbd[rows, c, :].rearrange("p (s h t) -> p s h t", h=8, t=4)[:, :, 2 * c + a_, :]
                src = qT[rows, c, 0:NT].rearrange("p (s t) -> p s t", t=4)
                S.op("dve", lambda e, dst=dst, src=src: e.tensor_copy(out=dst, in_=src),
                     reads=[R("qT")], writes=[R("Qbd")])

        s_loadK(0)
        s_loadV(0)
        s_loadK(1)
        s_prepV(0)
        s_prepK(0)
        for sq in range(NS):
            if sq + 1 < NS:
                s_loadV(sq + 1)
                s_prepK(sq + 1)
            s_comp(sq)
            if sq + 1 < NS:
                s_prepV(sq + 1)
            if sq + 2 < NS:
                s_loadK(sq + 2)
        acc = accs[0]
        S.op("dve", lambda e: e.tensor_copy(out=acc, in_=psOs), reads=[RB(7)], writes=r_acc[0])
        accv = acc.rearrange("p (s c a t) -> p c a s t", s=NS, c=4, a=2)
        Rv = Rb[:, 0:256].rearrange("p (c s t) -> p c s t", c=4, t=4)
        for c in range(4):
            S.op("dve", lambda e, c=c: e.reciprocal(out=Rv[0:64, c], in_=accv[64:128, c, 0]),
                 reads=r_acc[0], writes=[*r_kr])
            S.op("dve", lambda e, c=c: e.reciprocal(out=Rv[64:128, c], in_=accv[0:64, c, 1]),
                 reads=r_acc[0], writes=[*r_kr])
        for c in range(4):
            S.op("dve", lambda e, c=c: e.tensor_tensor(
                out=attnT[0:64, c, 0:NT].rearrange("p (s t) -> p s t", t=4), in0=accv[0:64, c, 0], in1=Rv[0:64, c],
                op=ALU.mult), reads=[*r_acc[0], *r_kr], writes=[R("attnT")])
            S.op("dve", lambda e, c=c: e.tensor_tensor(
                out=attnT[64:128, c, 0:NT].rearrange("p (s t) -> p s t", t=4), in0=accv[64:128, c, 1], in1=Rv[64:128, c],
                op=ALU.mult), reads=[*r_acc[0], *r_kr], writes=[R("attnT")])
        for c in range(4):
            S.op("act", lambda e, c=c: e.activation(out=sqB[:, 0:NT], in_=attnT[:, c, 0:NT], func=AF.Square),
                 reads=[R("attnT")], writes=[*r_sqB])
            S.op("pe", lambda e, c=c: e.matmul(bank(4)[:, 0:NT], lhsT=ones_bf[:, :], rhs=sqB[:, 0:NT],
                                               start=(c == 0), stop=(c == 3)),
                 reads=[*r_sqB, R("ones_bf")], writes=[RB(4)])
        S.op("act", lambda e: e.activation(out=rbB[:, 0:NT], in_=bank(4)[:, 0:NT], func=AF.Ln, scale=1.0 / 512,
                                           bias=epsc[:, :]), reads=[RB(4), R("epsc")], writes=[*r_vf])
        S.op("act", lambda e: e.activation(out=rbB[:, 0:NT], in_=rbB[:, 0:NT], func=AF.Exp, scale=-0.5),
             reads=[*r_vf], writes=[*r_vf])
        for c in range(4):
            S.op("dve", lambda e, c=c: e.scalar_tensor_tensor(
                out=attnT[:, c, 0:NT], in0=attnT[:, c, 0:NT], scalar=gmix[:, 0, c:c + 1], in1=rbB[:, 0:NT],
                op0=ALU.mult, op1=ALU.mult), reads=[R("attnT"), *r_vf, R("gmix")], writes=[R("attnT")])
        for fn in mlp_block(1, NT, lambda t: xs, lambda t: ys,
                            lambda c, t: (attnT if c < 4 else convT)[:, c % 4, 0:NT], NT):
            fn()

    for b in range(NSEQ):
        for g in range(4):
            if dbg is not None and (b, g) not in dbg["bg"]:
                continue
            ph = "ABC" if dbg is None else dbg["ph"]
            if "A" in ph:
                phase_A(b, g)
            if "B" in ph:
                phase_B(b, g)
            if "C" in ph:
                finish_conv()
                phase_C(b, g)
    if with_sample and (dbg is None or dbg.get("sample")):
        finish_conv()
        sample_all()

    if dbg is not None and dbg.get("dump") == "csg":
        S.op("pool", lambda e: e.dma_start(out=dbgo[:, 0:256], in_=csg[:, :, :].rearrange("p t c -> p (t c)")),
             reads=[R("csg")], dma=True)
    with ExitStack() as es2:
        sems = {e: es2.enter_context(nc.semaphore(f"sem_{e}")) for e in ENGS}
        dsems = {}
        for q in ("sp", "pool"):
            for k in range(S.ndma):
                dsems[(q, k)] = es2.enter_context(nc.semaphore(f"dsem_{q}{k}"))
        with nc.Block() as block:
            S.emit(nc, block, sems, dsems)
    es.close()
    return nc


def _host_consts():
    p = np.arange(128)[:, None]
    j = np.arange(128)[None, :]
    prev = np.where(j <= p, 0.0, NEG).astype(np.float32)
    diag = np.where(p <= j, 0.0, NEG).astype(np.float32)
    neg = np.full((128, 128), NEG, np.float32)
    v0 = np.concatenate([prev, diag, prev, diag], 1)
    v1 = np.concatenate([neg, diag, prev, diag], 1)
    v2 = np.concatenate([neg, diag, neg, diag], 1)
    mask12 = np.concatenate([v0, v1, v2], 1)
    m3 = []
    jj = np.arange(32)[None, :]
    for g in range(4):
        blk = np.where(p <= 32 * g + jj, 0.0, NEG).astype(np.float32)
        m3.append(np.tile(blk, (1, 16)))
    mask3 = np.concatenate(m3, 1)
    half = 32
    inv = (10000.0 ** (-np.arange(half, dtype=np.float32) * 2.0 / 64)).astype(np.float32)
    pos = np.arange(SEQ, dtype=np.float32)
    ang = pos[:, None] * inv[None, :]
    cs = np.concatenate([np.cos(ang), np.sin(ang)], 1).astype(np.float32)
    csp = cs.reshape(16, 128, 64).transpose(1, 0, 2).reshape(128, 16 * 64)
    poss = (PAST + np.arange(4)).astype(np.float32)
    angs = poss[:, None] * inv[None, :]
    css1 = np.concatenate([np.cos(angs), np.sin(angs)], 1).astype(np.float32)
    css = np.tile(css1, (NS, 1))
    smask = np.where(p >= np.arange(4)[None, :], 1.0, 0.0).astype(np.float32)
    wnew = np.zeros((64, 64), np.float32)
    for s in range(NS):
        for t in range(4):
            for t2 in range(t + 1):
                wnew[4 * s + t2, 4 * s + t] = 3.0 if t2 == t else 1.0
    smask = np.ascontiguousarray(np.tile(smask, (1, 8)))
    wnew = np.ascontiguousarray(np.tile(wnew, (1, 8)))
    return dict(ident=np.eye(128, dtype=np.float32), mask12=np.ascontiguousarray(mask12),
                mask3=np.ascontiguousarray(mask3), csp=np.ascontiguousarray(csp), css=css,
                smask=smask, wnew=wnew)


def kernel(x_prompt, x_sample, cache_k, cache_v, state_conv, n_att_pre, n_att_post, w_in, conv_w,
           g_att, g_conv, w_out, n_mlp_pre, n_mlp_post, w_up, w_down):
    f = lambda a: np.ascontiguousarray(np.asarray(a, dtype=np.float32))
    x_prompt, x_sample, cache_k, cache_v, state_conv = map(f, (x_prompt, x_sample, cache_k, cache_v, state_conv))
    consts = _host_consts()
    col = lambda v: f(v).reshape(-1, 128).T
    gpre = np.concatenate([col(n_att_pre[0]), col(n_mlp_pre[0])], 1)
    gpost = np.concatenate([np.broadcast_to(f(n_att_post[0])[None, :], (128, D)),
                            np.broadcast_to(f(n_mlp_post[0])[None, :], (128, D))], 1)
    gmix = np.concatenate([col(g_att[0]), col(g_conv[0])], 1)
    cw = f(conv_w[0])
    convw = cw.reshape(3, 4, 128).transpose(2, 1, 0).reshape(128, 12)
    shared = dict(w_in=f(w_in[0]), w_out=f(w_out[0]), w_up=f(w_up[0]), w_down=f(w_down[0]),
                  gpre=f(gpre), gpost=f(gpost), gmix=f(gmix), convw=f(convw), **consts)
    in_maps = []
    for c in range(NCORES):
        m = dict(shared)
        m["xp"] = x_prompt[NSEQ * c:NSEQ * (c + 1)]
        m["xs"] = x_sample[NS * c:NS * (c + 1)].reshape(64, D)
        m["ck"] = cache_k[0, NS * c:NS * (c + 1)].reshape(NS, 2048, 512)
        m["cv"] = cache_v[0, NS * c:NS * (c + 1)].reshape(NS, 2048, 512)
        m["scv"] = state_conv[0, NS * c:NS * (c + 1)]
        in_maps.append(m)
    nc = build()
    res = run_bass_kernel_spmd(nc, in_maps, core_ids=list(range(NCORES)))
    rs = res.results
    cat = lambda k: np.concatenate([np.asarray(r[k], dtype=np.float32) for r in rs], 0)
    y_p = cat("yp")
    y_s = cat("ys").reshape(128, 4, D)
    k_p = cat("kp").reshape(1, 16, SEQ, 8, 64)
    v_p = cat("vp").reshape(1, 16, SEQ, 8, 64)
    c_p = cat("cp").reshape(1, 16, 2, 512)
    k_s = cat("ks").reshape(1, 128, 4, 8, 64)
    v_s = cat("vs").reshape(1, 128, 4, 8, 64)
    c_s = cat("cso").reshape(1, 128, 2, 512)
    return (y_p, y_s, k_p, v_p, c_p, k_s, v_s, c_s)
```

```python
import numpy as np
from contextlib import ExitStack
import concourse.bass as bass
import concourse.mybir as mybir
from concourse.bass_utils import run_bass_kernel_spmd

F32 = mybir.dt.float32
BF16 = mybir.dt.bfloat16
AF = mybir.ActivationFunctionType
ALU = mybir.AluOpType

NCORES = 8
D = 1024
SEQ = 2048
NSEQ = 2
NS = 16
PAST = 8192
EPS = 1e-6
NEG = -30000.0
ENGS = ("pe", "act", "dve", "pool", "sp")


class Res:
    __slots__ = ("name", "w", "r")

    def __init__(self, name):
        self.name = name
        self.w = None
        self.r = []


class Sched:
    def __init__(self, ndma=20):
        self.ops = {e: [] for e in ENGS}
        self.known = {e: {} for e in ENGS}
        self.ndma = ndma
        self.dma_cnt = {("sp", k): 0 for k in range(ndma)}
        self.dma_cnt.update({("pool", k): 0 for k in range(ndma)})
        self.dma_rr = {"sp": 0, "pool": 0}
        self.dma_last = {}

    def op(self, eng, fn, reads=(), writes=(), dma=False):
        deps = []
        for r in reads:
            if r.w is not None:
                deps.append(r.w)
        for w in writes:
            if w.w is not None:
                deps.append(w.w)
            deps.extend(w.r)
        known = self.known[eng]
        idx = len(self.ops[eng])
        dkey = None
        if dma:
            k = self.dma_rr[eng]
            self.dma_rr[eng] = (k + 1) % self.ndma
            dkey = (eng, k)
            if dkey in self.dma_last:
                deps.append(self.dma_last[dkey])
        waits = []
        newknown = None
        for (key, seq, vc) in deps:
            cur = (newknown if newknown is not None else known)
            if cur.get(key, 0) >= seq:
                continue
            if key == "pe" and eng == "pe" and not dma:
                continue
            if newknown is None:
                newknown = dict(known)
            waits.append((key, seq))
            for kk, vv in vc.items():
                if newknown.get(kk, 0) < vv:
                    newknown[kk] = vv
            if newknown.get(key, 0) < seq:
                newknown[key] = seq
        if newknown is not None:
            self.known[eng] = newknown
            known = newknown
        wmax = {}
        for key, seq in waits:
            if wmax.get(key, 0) < seq:
                wmax[key] = seq
        rec = {"fn": fn, "waits": list(wmax.items()), "sig": False, "dma": dkey, "dseq": None}
        self.ops[eng].append(rec)
        if dma:
            self.dma_cnt[dkey] += 1
            rec["dseq"] = self.dma_cnt[dkey]
            tok = (dkey, rec["dseq"], known)
            self.dma_last[dkey] = tok
        else:
            tok = (eng, idx + 1, known)
        for r in reads:
            r.r.append(tok)
        for w in writes:
            w.w = tok
            w.r = []
        return tok

    def emit(self, nc, block, sems, dsems):
        for e in ENGS:
            for rec in self.ops[e]:
                for key, seq in rec["waits"]:
                    if isinstance(key, str):
                        self.ops[key][seq - 1]["sig"] = True
        sigval = {}
        for e in ENGS:
            c = 0
            for i, rec in enumerate(self.ops[e]):
                if rec["sig"]:
                    c += 1
                    sigval[(e, i + 1)] = c
        import sys
        print("SCHED ops", {e: len(self.ops[e]) for e in ENGS}, "signals",
              {e: sum(1 for r in self.ops[e] if r["sig"]) for e in ENGS},
              "waits", {e: sum(len(r["waits"]) for r in self.ops[e]) for e in ENGS}, file=sys.stderr)
        finals = []
        for dkey, cnt in self.dma_cnt.items():
            if cnt > 0:
                finals.append((dkey, cnt))

        def run(e, eng):
            for rec in self.ops[e]:
                for key, seq in rec["waits"]:
                    if isinstance(key, str):
                        eng.wait_ge(sems[key], sigval[(key, seq)])
                    else:
                        eng.wait_ge(dsems[key], 16 * seq)
                ins = rec["fn"](eng)
                if rec["dma"] is not None:
                    ins.then_inc(dsems[rec["dma"]], 16)
                elif rec["sig"]:
                    ins.then_inc(sems[e], 1)
            if e == "pool":
                for dkey, cnt in finals:
                    eng.wait_ge(dsems[dkey], 16 * cnt)

        block.tensor(lambda eng: run("pe", eng))
        block.scalar(lambda eng: run("act", eng))
        block.vector(lambda eng: run("dve", eng))
        block.gpsimd(lambda eng: run("pool", eng))
        block.sync(lambda eng: run("sp", eng))


def bcast_mid(ap2d, n):
    a = ap2d.ap
    return bass.AP(ap2d.tensor, ap2d.offset, [list(a[0]), [0, n], list(a[1])])


def bcast_last(ap2d, n):
    a = ap2d.ap
    return bass.AP(ap2d.tensor, ap2d.offset, [list(a[0]), list(a[1]), [0, n]])


def build(with_sample=True, dbg=None):
    nc = bass.Bass("TRN2", target_bir_lowering=False)

    def din(name, shape, dt=F32):
        return nc.dram_tensor(name, list(shape), dt, kind="ExternalInput").ap()

    def dout(name, shape, dt=F32):
        return nc.dram_tensor(name, list(shape), dt, kind="ExternalOutput").ap()

    def dscr(name, shape, dt):
        return nc.dram_tensor(name, list(shape), dt, kind="Internal").ap()

    xp = din("xp", [NSEQ, SEQ, D])
    xs = din("xs", [64, D])
    ck = din("ck", [NS, 2048, 512])
    cv = din("cv", [NS, 2048, 512])
    scv = din("scv", [NS, 2, 512])
    w_in = din("w_in", [D, 3072])
    w_out = din("w_out", [D, D])
    w_up = din("w_up", [D, 4096])
    w_down = din("w_down", [4096, D])
    gpre_d = din("gpre", [128, 16])
    gpost_d = din("gpost", [128, 2048])
    gmix_d = din("gmix", [128, 8])
    convw_d = din("convw", [128, 12])
    ident_d = din("ident", [128, 128])
    mask12_d = din("mask12", [128, 3 * 512])
    mask3_d = din("mask3", [128, 4 * 512])
    csp_d = din("csp", [128, 16 * 64])
    css_d = din("css", [64, 64])
    smask_d = din("smask", [128, 32])
    wnew_d = din("wnew", [64, 512])

    yp = dout("yp", [NSEQ, SEQ, D])
    ys = dout("ys", [64, D])
    kp = dout("kp", [NSEQ, SEQ, 512])
    vp = dout("vp", [NSEQ, SEQ, 512])
    cp = dout("cp", [NSEQ, 2, 512])
    ks = dout("ks", [64, 512])
    vs = dout("vs", [64, 512])
    cso = dout("cso", [NS, 2, 512])
    dbgo = dout("dbgo", [128, 1024]) if dbg is not None else None

    vscr = dscr("vscr", [NSEQ, SEQ, 768], BF16)
    wup_s = dscr("wup_s", [D, 4096], BF16)
    wdn_s = dscr("wdn_s", [4096, D], BF16)

    S = Sched()
    es = ExitStack()
    res = {}

    def R(name):
        if name not in res:
            res[name] = Res(name)
        return res[name]

    def sb(name, shape, dt):
        return es.enter_context(nc.sbuf_tensor(name, list(shape), dt))

    w_in_sb = sb("w_in_sb", [128, 8, 3072], BF16)
    w_out_sb = sb("w_out_sb", [128, 8, 1024], BF16)
    gpre = sb("gpre_sb", [128, 2, 8], F32)
    gpost = sb("gpost_sb", [128, 2, 1024], F32)
    gmix = sb("gmix_sb", [128, 2, 4], F32)
    convw = sb("convw_sb", [128, 4, 3], F32)
    ident_bf = sb("ident_bf", [128, 128], BF16)
    ident_f = sb("ident_f", [128, 128], F32)
    ones_bf = sb("ones_bf", [128, 128], BF16)
    mask12 = sb("mask12_sb", [128, 3, 512], BF16)
    mask3 = sb("mask3_sb", [128, 4, 512], BF16)
    csg = sb("csg", [128, 4, 64], F32)
    kT = sb("kT", [128, 4, 2048], BF16)
    qT = sb("qT", [128, 4, 512], BF16)
    Vnat = sb("Vnat", [128, 8, 768], BF16)
    Vr4 = sb("Vr4", [128, 8, 768], BF16)
    Vr16 = sb("Vr16", [128, 16, 768], BF16)
    convT = sb("convT", [128, 4, 512], BF16)
    attnT = sb("attnT", [128, 4, 512], BF16)
    small = sb("small", [128, 16], F32)
    s1 = sb("s1", [128, 2048], F32)
    s2 = sb("s2", [128, 2048], F32)
    s3 = sb("s3", [128, 2064], F32)
    s4 = sb("s4", [128, 1024], F32)
    s5 = sb("s5", [128, 1024], F32)
    s6 = sb("s6", [128, 1024], F32)
    s9 = sb("s9", [128, 1024], F32)
    junk = sb("junk", [128, 1024], BF16)
    cbuf = sb("cbuf", [128, 1024], F32)
    carry = sb("carry", [128, 4, 2], F32)
    epsc = sb("epsc", [128, 1], F32)
    P_s = sb("P_s", [128, 8, 12], BF16)
    P_s2 = sb("P_s2", [128, 8, 12], BF16)
    P_new = sb("P_new", [64, 8, 64], BF16)
    smask8 = sb("smask8", [128, 8, 4], BF16)
    wnew8 = sb("wnew8", [64, 8, 64], BF16)

    xring = [s1[:, 0:1024], s1[:, 1024:2048]]
    x1buf = [s1[:, 0:1024], s1[:, 1024:2048]]
    hT = s2[:, :].bitcast(BF16).rearrange("p (c n) -> p c n", c=8)
    wu_ring = [s2[:, 0:1024].bitcast(BF16).rearrange("p (c n) -> p c n", c=8),
               s2[:, 1024:2048].bitcast(BF16).rearrange("p (c n) -> p c n", c=8)]
    gi = s3[:, :].rearrange("p (j n) -> p j n", j=4)
    wd_ring = [s3[:, 0:1024].bitcast(BF16).rearrange("p (k n) -> p k n", k=2),
               s3[:, 1024:2048].bitcast(BF16).rearrange("p (k n) -> p k n", k=2)]
    hbf = s4[:, 0:512].bitcast(BF16)
    q_r = s4[:, 512:768].bitcast(BF16)
    k_rb = s4[:, 768:1024].bitcast(BF16)
    accs = [s4[:, 0:512], s4[:, 512:1024]]
    tmpC = s4[:, :]
    k_r = s5[:, 0:512]
    v_f = s5[:, 512:1024]
    Rb = s5[:, 0:512]
    rbB = s5[:, 512:1024]
    xr = s5[:, :]
    ropeA = s6[:, 0:256]
    ropeB = s6[:, 256:512]
    u_sb = s6[:, 512:1024]
    Pb = [s6[:, 0:256].bitcast(BF16), s6[:, 256:512].bitcast(BF16)]
    sqB = s6[:, 512:768].bitcast(BF16)
    h2 = s6[:, 0:512].bitcast(BF16)
    r_ring = [s6[:, 512:640].bitcast(BF16), s6[:, 640:768].bitcast(BF16)]
    r2_ring = [s6[:, 768:896].bitcast(BF16), s6[:, 896:1024].bitcast(BF16)]
    t0 = s9[:, 0:512]
    sqA = s9[:, 512:768].bitcast(BF16)
    rbA = s9[:, 512:1024]
    h2T = s9[:, 0:1024].bitcast(BF16).rearrange("p (c n) -> p c n", c=8)

    r_hT = [R("wu0"), R("wu1")]
    r_gi = [R("wd0"), R("wd1")]
    q0, q1, q2, q3 = R("s4q0"), R("s4q1"), R("s4q2"), R("s4q3")
    r_hbf, r_qr, r_krb = [q0, q1], [q2], [q3]
    r_acc = [[q0, q1], [q2, q3]]
    r_tmpC = [q0, q1, q2, q3]
    r_kr, r_vf = [R("s5a")], [R("s5b")]
    r_xr = [R("s5a"), R("s5b")]
    ee = [R(f"s6e{i}") for i in range(8)]
    r_ropeA, r_ropeB, r_usb = ee[0:2], ee[2:4], ee[4:8]
    r_P = [ee[0:2], ee[2:4]]
    r_sqB, r_h2 = ee[4:6], ee[0:4]
    r_r = [[ee[4]], [ee[5]]]
    r_r2 = [[ee[6]], [ee[7]]]
    r_t0, r_s9b = [R("s9a")], [R("s9b")]
    r_h2T = [R("s9a"), R("s9b")]
    sqB2 = cbuf[:, 0:256].bitcast(BF16)
    Pb4 = [Pb[0], Pb[1], cbuf[:, 512:768].bitcast(BF16), cbuf[:, 768:1024].bitcast(BF16)]
    r_P4 = [r_P[0], r_P[1], [R("cbufb")], [R("cbufb2")]]
    SBK = [0, 1, 5, 6]
    r_attnAll = [R("attnT")] + [R(f"attnT{i}") for i in range(4)]
    r_convAll = [R("convT")] + [R(f"convT{i}") for i in range(4)]
    pp = [es.enter_context(nc.psum_tensor(f"pp{i}", [128, 1024], F32)) for i in range(4)]

    def bank(k):
        return pp[k // 2][:, (k % 2) * 512:(k % 2) * 512 + 512]

    def RB(k):
        return R(f"bank{k}")

    S.op("pool", lambda e: e.dma_start(out=ident_bf[:, :], in_=ident_d), writes=[R("ident_bf")], dma=True)
    S.op("pool", lambda e: e.dma_start(out=mask12[:, :, :], in_=mask12_d.rearrange("p (v n) -> p v n", v=3)),
         writes=[R("mask12")], dma=True)
    S.op("pool", lambda e: e.dma_start(out=mask3[:, :, :], in_=mask3_d.rearrange("p (v n) -> p v n", v=4)),
         writes=[R("mask3")], dma=True)
    w_in_v = w_in.rearrange("(c p) n -> p c n", p=128)
    for j in range(6):
        S.op("pool", lambda e, j=j: e.dma_start(out=w_in_sb[:, :, j * 512:(j + 1) * 512],
                                                 in_=w_in_v[:, :, j * 512:(j + 1) * 512]),
             writes=[R(f"w_in{j}")], dma=True)
    w_out_v = w_out.rearrange("(c p) n -> p c n", p=128)
    S.op("pool", lambda e: e.dma_start(out=w_out_sb[:, :, :], in_=w_out_v), writes=[R("w_out")], dma=True)
    stg_aps = [Vr4[:, 4:8, :].rearrange("p t n -> p (t n)")[:, 0:2048],
               Vnat[:, 4:8, :].rearrange("p t n -> p (t n)")[:, 0:2048]]
    stg_rs = [[R("stgA")], [R("stgB")]]
    wup_v = w_up.rearrange("(c p) n -> p c n", p=128)
    wups_v = wup_s.rearrange("(c p) n -> p c n", p=128)
    wdn_v = w_down.rearrange("(k p) n -> p k n", p=128)
    wdns_v = wdn_s.rearrange("(k p) n -> p k n", p=128)
    r_wconv = [R(f"wconv{q}") for q in range(32)]
    cstate = {"i": 0, "done": False}

    def pump(n=1):
        for _ in range(n):
            q = cstate["i"]
            if q >= 32:
                return
            cstate["i"] += 1
            sl = q % 2
            if q < 16:
                src = wup_v[:, :, q * 256:(q + 1) * 256]
                dst = wups_v[:, :, q * 256:(q + 1) * 256]
                sv = stg_aps[sl].rearrange("p (c n) -> p c n", c=8)
            else:
                k = q - 16
                src = wdn_v[:, 2 * k:2 * k + 2, :]
                dst = wdns_v[:, 2 * k:2 * k + 2, :]
                sv = stg_aps[sl].rearrange("p (k n) -> p k n", k=2)
            if cstate.get("pend") is not None:
                cstate["pend"]()
            S.op("pool", lambda e, src=src, sv=sv: e.dma_start(out=sv, in_=src), writes=stg_rs[sl], dma=True)

            def out_dma(dst=dst, sv=sv, sl=sl, q=q):
                S.op("sp", lambda e: e.dma_start(out=dst, in_=sv), reads=stg_rs[sl], writes=[r_wconv[q]], dma=True)
            cstate["pend"] = out_dma

    def finish_conv():
        if cstate["done"]:
            return
        pump(32)
        if cstate.get("pend") is not None:
            cstate["pend"]()
            cstate["pend"] = None
        cstate["done"] = True
        for (vt, rs, nm) in ((Vr4, R("stgA"), "Vr4"), (Vnat, R("stgB"), "Vnat")):
            vv = vt[:, 4:8, :].rearrange("p t (h c) -> p t h c", h=4)
            S.op("pool", lambda e, vv=vv: e.memset(vv[:, :, :, 64:128], 1.0), writes=[rs, R(nm)])

    S.op("sp", lambda e: e.dma_start(out=gpre[:, :, :], in_=gpre_d.rearrange("p (a c) -> p a c", a=2)),
         writes=[R("gpre")], dma=True)
    S.op("sp", lambda e: e.dma_start(out=gpost[:, :, :], in_=gpost_d.rearrange("p (a c) -> p a c", a=2)),
         writes=[R("gpost")], dma=True)
    S.op("sp", lambda e: e.dma_start(out=gmix[:, :, :], in_=gmix_d.rearrange("p (a c) -> p a c", a=2)),
         writes=[R("gmix")], dma=True)
    S.op("sp", lambda e: e.dma_start(out=convw[:, :, :], in_=convw_d.rearrange("p (a c) -> p a c", a=4)),
         writes=[R("convw")], dma=True)
    S.op("sp", lambda e: e.dma_start(out=ident_f[:, :], in_=ident_d), writes=[R("ident_f")], dma=True)
    S.op("pool", lambda e: e.memset(ones_bf[:, :], 1.0), writes=[R("ones_bf")])
    S.op("pool", lambda e: e.memset(epsc[:, :], EPS), writes=[R("epsc")])
    for nm, vt, n in (("Vnat", Vnat, 8), ("Vr4", Vr4, 8), ("Vr16", Vr16, 16)):
        vv = vt[:, :, :].rearrange("p t (h c) -> p t h c", h=4)
        S.op("pool", lambda e, vv=vv: e.memset(vv[:, :, :, 64:128], 1.0), writes=[R(nm)])

    def rstd_ops(ss_ap, out_ap, n, rn):
        S.op("act", lambda e: e.activation(out=out_ap, in_=ss_ap, func=AF.Ln, scale=1.0 / n, bias=epsc[0:ss_ap.shape[0], :]),
             reads=[rn, R("epsc")], writes=[rn])
        S.op("act", lambda e: e.activation(out=out_ap, in_=out_ap, func=AF.Exp, scale=-0.5), reads=[rn], writes=[rn])

    def phase_A(b, g):
        S.op("sp", lambda e: e.dma_start(out=csg[:, :, :],
                                         in_=csp_d.rearrange("p (t c) -> p t c", t=16)[:, 4 * g:4 * g + 4, :]),
             writes=[R("csg")], dma=True)

        def head(t):
            i = 4 * g + t
            tok0 = 128 * i
            xt = xring[t % 2]
            rx = R(f"s1_{t % 2}")
            qb = (1, 2, 3) if t % 2 == 0 else (5, 6, 7)
            S.op("sp", lambda e: e.dma_start(out=xt, in_=xp[b, tok0:tok0 + 128, :]), writes=[rx], dma=True)
            S.op("act", lambda e: e.activation(out=junk[:, :], in_=xt, func=AF.Square, accum_out=small[:, 0:1]),
                 reads=[rx], writes=[R("junk"), R("ss0")])
            rstd_ops(small[:, 0:1], small[:, 1:2], 1024.0, R("ss0"))
            S.op("dve", lambda e: e.tensor_scalar(out=hbf, in0=xt, scalar1=small[:, 1:2], scalar2=None,
                                                  op0=ALU.mult), reads=[rx, R("ss0")], writes=[*r_hbf])
            psT = bank(0).bitcast(BF16)
            for c in range(8):
                S.op("pe", lambda e, c=c: e.transpose(out=psT[:, c * 128:(c + 1) * 128],
                                                      in_=hbf[:, c * 128:(c + 1) * 128], identity=ident_bf[:, :]),
                     reads=[*r_hbf, R("ident_bf")], writes=[RB(0)])
            S.op("dve", lambda e: e.tensor_tensor(
                out=hT[:, :, t * 128:(t + 1) * 128], in0=psT.rearrange("p (c n) -> p c n", c=8),
                in1=bcast_last(gpre[:, 0, :], 128), op=ALU.mult),
                reads=[RB(0), R("gpre")], writes=[R(f"hT{t}"), *r_hT])
            for n in range(3):
                for c in range(8):
                    S.op("pe", lambda e, n=n, c=c: e.matmul(
                        bank(qb[n]), lhsT=hT[:, c, t * 128:(t + 1) * 128],
                        rhs=w_in_sb[:, c, n * 512:(n + 1) * 512], start=(c == 0), stop=(c == 7)),
                        reads=[R(f"hT{t}"), *r_hT, R(f"w_in{n}")], writes=[RB(qb[n])])

        def tail(t):
            i = 4 * g + t
            tok0 = 128 * i
            qb = (1, 2, 3) if t % 2 == 0 else (5, 6, 7)
            cosb = bcast_mid(csg[:, t, 0:32], 8)
            sinb = bcast_mid(csg[:, t, 32:64], 8)
            tA = ropeA.rearrange("p (h d) -> p h d", h=8)
            tB = ropeB.rearrange("p (h d) -> p h d", h=8)
            tC = cbuf[:, 512:768].rearrange("p (h d) -> p h d", h=8)
            tD = cbuf[:, 768:1024].rearrange("p (h d) -> p h d", h=8)
            r_tC, r_tD = [R("cbufb")], [R("cbufb2")]

            def rope(src_bank, dst, rdst, fin):
                src = bank(src_bank).rearrange("p (h two d) -> p h two d", h=8, two=2)
                dv = dst.rearrange("p (h two d) -> p h two d", h=8, two=2)
                rsrc = RB(src_bank)
                S.op("dve", lambda e: e.tensor_tensor(out=tA, in0=src[:, :, 0, :], in1=cosb, op=ALU.mult),
                     reads=[rsrc, R("csg")], writes=[*r_ropeA])
                S.op("dve", lambda e: e.tensor_tensor(out=tB, in0=src[:, :, 1, :], in1=sinb, op=ALU.mult),
                     reads=[rsrc, R("csg")], writes=[*r_ropeB])
                S.op("dve", lambda e: e.tensor_tensor(out=tC, in0=src[:, :, 1, :], in1=cosb, op=ALU.mult),
                     reads=[rsrc, R("csg")], writes=r_tC)
                S.op("dve", lambda e: e.tensor_tensor(out=tD, in0=src[:, :, 0, :], in1=sinb, op=ALU.mult),
                     reads=[rsrc, R("csg")], writes=r_tD)
                S.op(fin, lambda e: e.tensor_tensor(out=dv[:, :, 0, :], in0=tA, in1=tB, op=ALU.subtract),
                     reads=[*r_ropeA, *r_ropeB], writes=rdst)
                S.op(fin, lambda e: e.tensor_tensor(out=dv[:, :, 1, :], in0=tC, in1=tD, op=ALU.add),
                     reads=[*r_tC, *r_tD], writes=rdst)

            S.op("act", lambda e: e.activation(out=v_f, in_=bank(qb[2]), func=AF.Copy), reads=[RB(qb[2])],
                 writes=[R("s5b")])
            rope(qb[0], q_r, r_qr, "dve")
            rope(qb[1], k_r, r_kr, "pool")
            S.op("act", lambda e: e.activation(out=k_rb, in_=k_r, func=AF.Copy), reads=[R("s5a")], writes=[*r_krb])
            slot = i % 8
            vdst = Vnat[:, slot, :].rearrange("p (h c) -> p h c", h=4)
            vsrc = v_f.rearrange("p (h a d) -> p h a d", h=4, a=2)
            S.op("pool", lambda e: e.tensor_copy(out=vdst[:, :, 0:64], in_=vsrc[:, :, 0, :]),
                 reads=[R("s5b")], writes=[R("Vnat")])
            S.op("pool", lambda e: e.tensor_copy(out=vdst[:, :, 128:192], in_=vsrc[:, :, 1, :]),
                 reads=[R("s5b")], writes=[R("Vnat")])
            S.op("pool", lambda e: e.dma_start(out=kp[b, tok0:tok0 + 128, :], in_=k_r), reads=[R("s5a")], dma=True)
            S.op("pool", lambda e: e.dma_start(out=vp[b, tok0:tok0 + 128, :], in_=v_f), reads=[R("s5b")], dma=True)
            S.op("pool", lambda e: e.dma_start(out=vscr[b, tok0:tok0 + 128, :], in_=Vnat[:, slot, :]),
                 reads=[R("Vnat")], writes=[R("vscr")], dma=True)
            psQ = bank(4).bitcast(BF16)
            for c in range(4):
                S.op("pe", lambda e, c=c: e.transpose(out=psQ[:, c * 128:(c + 1) * 128],
                                                      in_=q_r[:, c * 128:(c + 1) * 128], identity=ident_bf[:, :]),
                     reads=[*r_qr, R("ident_bf")], writes=[RB(4)])
            for c in range(4):
                S.op("pe", lambda e, c=c: e.transpose(out=psQ[:, 512 + c * 128:512 + (c + 1) * 128],
                                                      in_=k_rb[:, c * 128:(c + 1) * 128], identity=ident_bf[:, :]),
                     reads=[*r_krb, R("ident_bf")], writes=[RB(4)])
            S.op("act", lambda e: e.activation(out=qT[:, :, t * 128:(t + 1) * 128],
                                               in_=psQ[:, 0:512].rearrange("p (c n) -> p c n", c=4),
                                               func=AF.Copy), reads=[RB(4)], writes=[R("qT")])
            S.op("act", lambda e: e.activation(out=kT[:, :, tok0:tok0 + 128],
                                               in_=psQ[:, 512:1024].rearrange("p (c n) -> p c n", c=4),
                                               func=AF.Copy), reads=[RB(4)], writes=[R("kT")])

        head(0)
        pump(1)
        for t in range(4):
            if t + 1 < 4:
                head(t + 1)
                pump(1)
            tail(t)
            pump(1)
        r_hTall = [R("hT0"), R("hT1"), R("hT2"), R("hT3")]
        if g == 0:
            S.op("pool", lambda e: e.memset(gi[:, :, 0:2], 0.0), writes=[*r_gi])
        else:
            S.op("pool", lambda e: e.tensor_copy(out=gi[:, :, 0:2], in_=carry[:, :, :]),
                 reads=[R("carry")], writes=[*r_gi])

        def conv_mm(j):
            bks = (1, 2, 3) if j % 2 == 0 else (5, 6, 7)
            for (bk, col0) in ((bks[0], 2048), (bks[1], 2560), (bks[2], 1536)):
                for c in range(8):
                    S.op("pe", lambda e, bk=bk, col0=col0, c=c: e.matmul(
                        bank(bk), lhsT=w_in_sb[:, c, col0 + j * 128:col0 + (j + 1) * 128],
                        rhs=hT[:, c, :], start=(c == 0), stop=(c == 7)),
                        reads=[*r_hTall, *r_hT, R(f"w_in{col0 // 512}")], writes=[RB(bk)])

        def conv_ew(j):
            bks = (1, 2, 3) if j % 2 == 0 else (5, 6, 7)
            ub = u_sb if j % 2 == 0 else cbuf[:, 0:512]
            tb = t0 if j % 2 == 0 else cbuf[:, 512:1024]
            rub = r_usb if j % 2 == 0 else [R("cbufa")]
            rtb = r_t0 if j % 2 == 0 else [R("cbufb"), R("cbufb2")]
            S.op("act", lambda e: e.activation(out=ub, in_=bank(bks[1]), func=AF.Copy), reads=[RB(bks[1])], writes=rub)
            S.op("dve", lambda e: e.tensor_tensor(out=gi[:, j, 2:514], in0=bank(bks[0]), in1=ub, op=ALU.mult),
                 reads=[RB(bks[0]), *rub], writes=[R(f"gi{j}")])
            S.op("act", lambda e: e.activation(out=tb, in_=gi[:, j, 0:512], func=AF.Copy, scale=convw[:, j, 0:1]),
                 reads=[R(f"gi{j}"), *r_gi, R("convw")], writes=rtb)
            for tap in (1, 2):
                S.op("dve", lambda e, tap=tap: e.scalar_tensor_tensor(
                    out=tb, in0=gi[:, j, tap:tap + 512], scalar=convw[:, j, tap:tap + 1], in1=tb,
                    op0=ALU.mult, op1=ALU.add), reads=[R(f"gi{j}"), *r_gi, R("convw"), *rtb], writes=rtb)
            S.op("dve", lambda e: e.tensor_tensor(out=convT[:, j, :], in0=bank(bks[2]), in1=tb, op=ALU.mult),
                 reads=[RB(bks[2]), *rtb], writes=[R(f"convT{j}")])
            S.op("act", lambda e: e.activation(out=sqA, in_=convT[:, j, :], func=AF.Square),
                 reads=[R(f"convT{j}")], writes=r_s9b)
            S.op("pe", lambda e: e.matmul(bank(4), lhsT=ones_bf[:, :], rhs=sqA, start=(j == 0), stop=(j == 3)),
                 reads=[*r_s9b, R("ones_bf")], writes=[RB(4)])

        conv_mm(0)
        for j in range(4):
            if j + 1 < 4:
                conv_mm(j + 1)
            conv_ew(j)
        r_cTall = [R("convT0"), R("convT1"), R("convT2"), R("convT3"), R("convT")]
        S.op("act", lambda e: e.activation(out=rbA, in_=bank(4), func=AF.Ln, scale=1.0 / 512, bias=epsc[:, :]),
             reads=[RB(4), R("epsc")], writes=r_s9b)
        S.op("act", lambda e: e.activation(out=rbA, in_=rbA, func=AF.Exp, scale=-0.5), reads=r_s9b, writes=r_s9b)
        for j in range(4):
            S.op("dve", lambda e, j=j: e.scalar_tensor_tensor(
                out=convT[:, j, :], in0=convT[:, j, :], scalar=gmix[:, 1, j:j + 1], in1=rbA,
                op0=ALU.mult, op1=ALU.mult), reads=[*r_cTall, *r_s9b, R("gmix")], writes=r_cTall)
        S.op("pool", lambda e: e.tensor_copy(out=carry[:, :, :], in_=gi[:, :, 512:514]),
             reads=[*r_gi, R("gi0"), R("gi1"), R("gi2"), R("gi3")], writes=[R("carry")])
        if g == 3:
            for j in range(4):
                S.op("pool", lambda e, j=j: e.dma_start(
                    out=cp[b, :, j * 128:(j + 1) * 128].rearrange("t p -> p t"),
                    in_=carry[:, j, :], allow_slow_non_contiguous=True),
                    reads=[R("carry")], dma=True)
        base4 = (g % 2) * 4
        src4 = vscr[b, 512 * g:512 * g + 512, :].rearrange("(m r) n -> m r n", r=4)
        S.op("sp", lambda e: e.dma_start(out=Vr4[:, base4:base4 + 4, :], in_=src4),
             reads=[R("vscr")], writes=[R("Vr4")], dma=True)
        src16 = vscr[b].rearrange("(m r) n -> m r n", r=16)[32 * g:32 * g + 32]
        S.op("sp", lambda e: e.dma_start(out=Vr16[32 * g:32 * g + 32, :, :], in_=src16),
             reads=[R("vscr")], writes=[R("Vr16")], dma=True)

    def phase_B(b, g):
        units = []
        for hp in range(4):
            for ab in range(2):
                rows = slice(64 * ab, 64 * ab + 64)
                vcol = hp * 192 + 64 * ab
                for half in range(2):
                    blocks = []
                    for qb in range(2):
                        i = 4 * g + 2 * half + qb
                        qc = (2 * half + qb) * 128
                        qap = qT[rows, hp, qc:qc + 128]
                        prev = None if i == 0 else (kT[rows, hp, 128 * (i - 1):128 * i],
                                                    Vnat[:, (i - 1) % 8, vcol:vcol + 128])
                        diag = (kT[rows, hp, 128 * i:128 * i + 128], Vnat[:, i % 8, vcol:vcol + 128])
                        blocks.append((qap, prev, diag, qc))
                    variant = 1 if (g == 0 and half == 0) else 0
                    units.append(dict(kind=12, hp=hp, pair_last=False, ab=ab, blocks=blocks, variant=variant, otile=0,
                                      first=(half == 0), last=(half == 1), vres="Vnat"))
                for half in range(2):
                    blocks = []
                    for rr in range(2):
                        r = 2 * half + rr
                        qap = qT[rows, hp, r:512:4]
                        prev = None if g == 0 else (kT[rows, hp, 512 * (g - 1) + r:512 * g:4],
                                                    Vr4[:, ((g - 1) % 2) * 4 + r, vcol:vcol + 128])
                        diag = (kT[rows, hp, 512 * g + r:512 * (g + 1):4], Vr4[:, (g % 2) * 4 + r, vcol:vcol + 128])
                        blocks.append((qap, prev, diag, r * 128))
                    variant = 2 if g == 0 else 0
                    units.append(dict(kind=12, hp=hp, pair_last=False, ab=ab, blocks=blocks, variant=variant, otile=1,
                                      first=(half == 0), last=(half == 1), vres="Vr4"))
                units.append(dict(kind=3, hp=hp, pair_last=(ab == 1), ab=ab, otile=2, first=True, last=True, rows=rows, vcol=vcol))

        def s_fn(k):
            u = units[k]
            hp = u["hp"]
            psS = bank(SBK[k % 4])
            rS = RB(SBK[k % 4])
            if u["kind"] == 12:
                S.op("pe", lambda e: e.matmul(psS, lhsT=ident_bf[:, :], rhs=mask12[:, u["variant"], :],
                                              start=True, stop=False),
                     reads=[R("ident_bf"), R("mask12")], writes=[rS])
                nb = len(u["blocks"])
                for bi, (qap, prev, diag, oc) in enumerate(u["blocks"]):
                    if prev is not None:
                        S.op("pe", lambda e, bi=bi, qap=qap, prev=prev: e.matmul(
                            psS[:, bi * 256:bi * 256 + 128], lhsT=prev[0], rhs=qap, start=False, stop=False),
                            reads=[R("kT"), R("qT")], writes=[rS])
                    S.op("pe", lambda e, bi=bi, qap=qap, diag=diag: e.matmul(
                        psS[:, bi * 256 + 128:bi * 256 + 256], lhsT=diag[0], rhs=qap, start=False,
                        stop=(bi == nb - 1)), reads=[R("kT"), R("qT")], writes=[rS])
            else:
                M = 32 * (g + 1)
                rows = u["rows"]
                S.op("pe", lambda e: e.matmul(psS[0:M, :], lhsT=ident_bf[:, 0:M], rhs=mask3[:, g, :],
                                              start=True, stop=False),
                     reads=[R("ident_bf"), R("mask3")], writes=[rS])
                for r16 in range(16):
                    S.op("pe", lambda e, r16=r16: e.matmul(
                        psS[0:M, r16 * 32:r16 * 32 + 32], lhsT=kT[rows, hp, r16:r16 + 16 * (M - 1) + 1:16],
                        rhs=qT[rows, hp, r16:512:16], start=False, stop=(r16 == 15)),
                        reads=[R("kT"), R("qT")], writes=[rS])

        def e_fn(k):
            u = units[k]
            psS = bank(SBK[k % 4])
            P = Pb4[k % 4]
            M = 128 if u["kind"] == 12 else 32 * (g + 1)
            S.op("act", lambda e: e.activation(out=P[0:M, :], in_=psS[0:M, :], func=AF.Exp, scale=0.125),
                 reads=[RB(SBK[k % 4])], writes=r_P4[k % 4])

        def pv_fn(k):
            u = units[k]
            hp = u["hp"]
            P = Pb4[k % 4]
            rP = r_P4[k % 4]
            ob = 2 + (u["otile"] + u["ab"]) % 2
            psO = bank(ob)
            rO = RB(ob)
            acc = accs[u["ab"]]
            racc = r_acc[u['ab']]
            if u["kind"] == 12:
                for bi, (qap, prev, diag, oc) in enumerate(u["blocks"]):
                    if prev is not None:
                        S.op("pe", lambda e, bi=bi, prev=prev, oc=oc: e.matmul(
                            psO[:, oc:oc + 128], lhsT=prev[1], rhs=P[:, bi * 256:bi * 256 + 128],
                            start=True, stop=False), reads=[*rP, R(u["vres"])], writes=[rO])
                    S.op("pe", lambda e, bi=bi, diag=diag, oc=oc, prev=prev: e.matmul(
                        psO[:, oc:oc + 128], lhsT=diag[1], rhs=P[:, bi * 256 + 128:bi * 256 + 256],
                        start=(prev is None), stop=True), reads=[*rP, R(u["vres"])], writes=[rO])
                if u["last"]:
                    if u["otile"] == 0:
                        S.op("dve", lambda e: e.tensor_copy(out=acc, in_=psO), reads=[rO], writes=racc)
                    else:
                        av = acc.rearrange("p (m r) -> p r m", r=4)
                        S.op("dve", lambda e: e.tensor_tensor(out=av, in0=psO.rearrange("p (r m) -> p r m", r=4),
                                                              in1=av, op=ALU.add),
                             reads=[rO, *racc], writes=racc)
            else:
                M = 32 * (g + 1)
                vcol = u["vcol"]
                for r16 in range(16):
                    S.op("pe", lambda e, r16=r16: e.matmul(
                        psO[:, r16 * 32:r16 * 32 + 32], lhsT=Vr16[0:M, r16, vcol:vcol + 128],
                        rhs=P[0:M, r16 * 32:r16 * 32 + 32], start=True, stop=True),
                        reads=[*rP, R("Vr16")], writes=[rO])
                av = acc.rearrange("p (j r) -> p r j", r=16)
                S.op("dve", lambda e: e.tensor_tensor(out=av, in0=psO.rearrange("p (r j) -> p r j", r=16),
                                                      in1=av, op=ALU.add), reads=[rO, *racc], writes=racc)

        def finish_pair(hp):
            S.op("dve", lambda e: e.reciprocal(out=Rb[0:64, :], in_=accs[0][64:128, :]),
                 reads=[*r_acc[0]], writes=[*r_kr])
            S.op("dve", lambda e: e.reciprocal(out=Rb[64:128, :], in_=accs[1][0:64, :]),
                 reads=[*r_acc[1]], writes=[*r_kr])
            S.op("dve", lambda e: e.tensor_tensor(out=attnT[0:64, hp, :], in0=accs[0][0:64, :],
                                                  in1=Rb[0:64, :], op=ALU.mult),
                 reads=[*r_acc[0], *r_kr], writes=[R(f"attnT{hp}"), R("attnT")])
            S.op("dve", lambda e: e.tensor_tensor(out=attnT[64:128, hp, :], in0=accs[1][64:128, :],
                                                  in1=Rb[64:128, :], op=ALU.mult),
                 reads=[*r_acc[1], *r_kr], writes=[R(f"attnT{hp}")])
            sq = sqB if hp % 2 == 0 else sqB2
            rsq = r_sqB if hp % 2 == 0 else [R("cbufa")]
            S.op("pool", lambda e: e.tensor_tensor(out=sq, in0=attnT[:, hp, :], in1=attnT[:, hp, :], op=ALU.mult),
                 reads=[R(f"attnT{hp}")], writes=rsq)

            def later():
                S.op("pe", lambda e: e.matmul(bank(4), lhsT=ones_bf[:, :], rhs=sq, start=(hp == 0), stop=(hp == 3)),
                     reads=[*rsq, R("ones_bf")], writes=[RB(4)])
            return later

        if dbg is not None and dbg.get("skip"):
            units[:] = [u for u in units if (u["otile"] not in dbg["skip"])]
        n = len(units)
        pending = []
        for k0 in range(min(3, n)):
            s_fn(k0)
        for k in range(n):
            if k + 3 < n:
                s_fn(k + 3)
            e_fn(k)
            pv_fn(k)
            pump(1)
            for item in list(pending):
                item[0] -= 1
                if item[0] <= 0:
                    item[1]()
                    pending.remove(item)
            if units[k]["pair_last"]:
                pending.append([3, finish_pair(units[k]["hp"])])
        for item in pending:
            item[1]()
        S.op("act", lambda e: e.activation(out=rbB, in_=bank(4), func=AF.Ln, scale=1.0 / 512, bias=epsc[:, :]),
             reads=[RB(4), R("epsc")], writes=[*r_vf])
        S.op("act", lambda e: e.activation(out=rbB, in_=rbB, func=AF.Exp, scale=-0.5), reads=[*r_vf], writes=[*r_vf])
        for hp in range(4):
            S.op("dve", lambda e, hp=hp: e.scalar_tensor_tensor(
                out=attnT[:, hp, :], in0=attnT[:, hp, :], scalar=gmix[:, 0, hp:hp + 1], in1=rbB,
                op0=ALU.mult, op1=ALU.mult), reads=[R(f"attnT{hp}"), *r_vf, R("gmix")], writes=[R(f"attnT{hp}"), R("attnT")])

    def norm_token_major(src_ap, rsrc, ntok, col):
        S.op("act", lambda e: e.activation(out=junk[0:ntok, :], in_=src_ap, func=AF.Square,
                                           accum_out=small[0:ntok, col:col + 1]),
             reads=rsrc, writes=[R("junk"), R(f"ss{col}")])
        rstd_ops(small[0:ntok, col:col + 1], small[0:ntok, col + 1:col + 2], 1024.0, R(f"ss{col}"))

    def mlp_block(ntiles, ntok, x_src_fn, y_dst_fn, cat_fn, tokw):
        def wload(f):
            sl = (f // 2) % 2
            S.op("sp", lambda e: e.dma_start(
                out=wu_ring[sl], in_=wup_s.rearrange("(c p) n -> p c n", p=128)[:, :, f * 128:f * 128 + 256]),
                reads=r_wconv[0:16], writes=[R(f"wu{sl}")], dma=True)
            S.op("sp", lambda e: e.dma_start(
                out=wd_ring[sl], in_=wdn_s[f * 128:f * 128 + 256, :].rearrange("(k p) n -> p k n", p=128)),
                reads=r_wconv[16:32], writes=[R(f"wd{sl}")], dma=True)

        def head_mm():
            wload(0)
            wload(2)
            for cs_ in ((4, 5, 6, 7), (0, 1, 2, 3)):
                for t in range(ntiles):
                    mixed = pp[2 + t]
                    for half in range(2):
                        for c in cs_:
                            S.op("pe", lambda e, half=half, c=c, t=t, mixed=mixed: e.matmul(
                                mixed[0:ntok, half * 512:(half + 1) * 512], lhsT=cat_fn(c, t),
                                rhs=w_out_sb[:, c, half * 512:(half + 1) * 512], start=(c == 4), stop=(c == 3)),
                                reads=[*(r_attnAll if c < 4 else r_convAll), R("w_out")],
                                writes=[RB(4 + 2 * t + half)])

        qT_f = qT[:, :, :].rearrange("p c n -> p (c n)").bitcast(F32)
        tb = [dict(tmp=tmpC, rtmp=r_tmpC, xr=xr, rxr=[R("s5a"), R("s5b")], h2=h2, rh2=r_h2, cols=(2, 4, 6)),
              dict(tmp=cbuf[:, :], rtmp=[R("cbufa"), R("cbufb"), R("cbufb2")], xr=qT_f, rxr=[R("qT")],
                   h2=s6[:, 512:1024].bitcast(BF16), rh2=ee[4:8], cols=(8, 10, 12))]

        def head_ew():
            for t in range(ntiles):
                B_ = tb[t]
                mixed = pp[2 + t]
                rmix = [RB(4 + 2 * t), RB(5 + 2 * t)]
                c1, c2 = B_["cols"][0], B_["cols"][1]
                S.op("sp", lambda e, t=t, B_=B_: e.dma_start(out=B_["xr"][0:ntok, :], in_=x_src_fn(t)), writes=B_["rxr"],
                     dma=True)
                norm_token_major(mixed[0:ntok, :], rmix, ntok, c1)
                S.op("dve", lambda e, mixed=mixed, B_=B_, c1=c1: e.scalar_tensor_tensor(
                    out=B_["tmp"][0:ntok, :], in0=mixed[0:ntok, :], scalar=small[0:ntok, c1 + 1:c1 + 2],
                    in1=gpost[0:ntok, 0, :], op0=ALU.mult, op1=ALU.mult),
                    reads=[*rmix, R(f"ss{c1}"), R("gpost")], writes=B_["rtmp"])
                x1 = x1buf[t]
                rx1 = R(f"s1_{t}")
                S.op("pool", lambda e, x1=x1, B_=B_: e.tensor_tensor(out=x1[0:ntok, :], in0=B_["tmp"][0:ntok, :],
                                                                     in1=B_["xr"][0:ntok, :], op=ALU.add),
                     reads=[*B_["rtmp"], *B_["rxr"]], writes=[rx1])
                norm_token_major(x1[0:ntok, :], [rx1], ntok, c2)
                S.op("dve", lambda e, x1=x1, B_=B_, c2=c2: e.tensor_scalar(
                    out=B_["h2"][0:ntok, :], in0=x1[0:ntok, :], scalar1=small[0:ntok, c2 + 1:c2 + 2], scalar2=None,
                    op0=ALU.mult), reads=[rx1, R(f"ss{c2}")], writes=B_["rh2"])
                tb_ = 0 if t == 0 else 2
                psT = bank(tb_).bitcast(BF16)
                for c in range(8):
                    S.op("pe", lambda e, c=c, psT=psT, B_=B_: e.transpose(out=psT[:, c * ntok:(c + 1) * ntok],
                                                                          in_=B_["h2"][0:ntok, c * 128:(c + 1) * 128],
                                                                          identity=ident_bf[0:ntok, 0:ntok]),
                         reads=[*B_["rh2"], R("ident_bf")], writes=[RB(tb_)])
                S.op("dve", lambda e, t=t, psT=psT: e.tensor_tensor(
                    out=h2T[:, :, t * ntok:(t + 1) * ntok], in0=psT[:, 0:8 * ntok].rearrange("p (c n) -> p c n", c=8),
                    in1=bcast_last(gpre[:, 1, :], ntok), op=ALU.mult),
                    reads=[RB(tb_), R("gpre")], writes=r_h2T)
        W = ntiles * ntok

        def up(f):
            sl = (f // 2) % 2
            psU = bank(6 + f % 2)
            for c in range(8):
                S.op("pe", lambda e, c=c: e.matmul(psU[:, 0:W], lhsT=wu_ring[sl][:, c, (f % 2) * 128:(f % 2) * 128 + 128],
                                                   rhs=h2T[:, c, 0:W], start=(c == 0), stop=(c == 7)),
                     reads=[R(f"wu{sl}"), *r_h2T], writes=[RB(6 + f % 2)])
            S.op("act", lambda e: e.activation(out=r_ring[f % 2][:, 0:W], in_=psU[:, 0:W], func=AF.Relu),
                 reads=[RB(6 + f % 2)], writes=r_r[f % 2])
            S.op("pool", lambda e: e.tensor_tensor(out=r2_ring[f % 2][:, 0:W], in0=r_ring[f % 2][:, 0:W],
                                                   in1=r_ring[f % 2][:, 0:W], op=ALU.mult),
                 reads=r_r[f % 2], writes=r_r2[f % 2])

        def down(f):
            sl = (f // 2) % 2
            for t in range(ntiles):
                for half in range(2):
                    S.op("pe", lambda e, t=t, half=half: e.matmul(
                        pp[t][0:ntok, half * 512:(half + 1) * 512], lhsT=r2_ring[f % 2][:, t * ntok:(t + 1) * ntok],
                        rhs=wd_ring[sl][:, f % 2, half * 512:(half + 1) * 512], start=(f == 0), stop=(f == 31)),
                        reads=[*r_r2[f % 2], R(f"wd{sl}")], writes=[RB(2 * t + half)])

        def loop():
            up(0)
            for f in range(32):
                if f + 1 < 32:
                    up(f + 1)
                down(f)
                if f % 2 == 1 and f + 3 < 32:
                    wload(f + 3)
        def tail():
            for t in range(ntiles):
                B_ = tb[t]
                c3 = B_["cols"][2]
                fp = pp[t]
                norm_token_major(fp[0:ntok, :], [RB(2 * t), RB(2 * t + 1)], ntok, c3)
                S.op("dve", lambda e, fp=fp, B_=B_, c3=c3: e.scalar_tensor_tensor(
                    out=B_["tmp"][0:ntok, :], in0=fp[0:ntok, :], scalar=small[0:ntok, c3 + 1:c3 + 2],
                    in1=gpost[0:ntok, 1, :], op0=ALU.mult, op1=ALU.mult),
                    reads=[RB(2 * t), RB(2 * t + 1), R(f"ss{c3}"), R("gpost")], writes=B_["rtmp"])
                x1 = x1buf[t]
                rx1 = R(f"s1_{t}")
                S.op("pool", lambda e, x1=x1, B_=B_: e.tensor_tensor(out=B_["xr"][0:ntok, :], in0=B_["tmp"][0:ntok, :],
                                                                     in1=x1[0:ntok, :], op=ALU.add),
                     reads=[*B_["rtmp"], rx1], writes=B_["rxr"])
                S.op("pool", lambda e, t=t, B_=B_: e.dma_start(out=y_dst_fn(t), in_=B_["xr"][0:ntok, :]),
                     reads=B_["rxr"], dma=True)
        return head_mm, head_ew, loop, tail

    def phase_C(b, g):
        parts = []
        for sub in range(2):
            def x_src(t, sub=sub):
                i = 4 * g + 2 * sub + t
                return xp[b, 128 * i:128 * i + 128, :]

            def y_dst(t, sub=sub):
                i = 4 * g + 2 * sub + t
                return yp[b, 128 * i:128 * i + 128, :]

            def cat(c, t, sub=sub):
                tc = (2 * sub + t) * 128
                return (attnT if c < 4 else convT)[:, c % 4, tc:tc + 128]

            parts.append(mlp_block(2, 128, x_src, y_dst, cat, 256))
        p0, p1 = parts
        p0[0](); p0[1](); p0[2]()
        p1[0]()
        p0[3]()
        p1[1](); p1[2](); p1[3]()

    def sample_all():
        NT = 64
        S.op("pool", lambda e: e.dma_start(out=smask8[:, :, :], in_=smask_d.rearrange("p (h t) -> p h t", h=8)),
             writes=[R("smask8")], dma=True)
        S.op("pool", lambda e: e.dma_start(out=wnew8[:, :, :], in_=wnew_d.rearrange("p (h t) -> p h t", h=8)),
             writes=[R("wnew8")], dma=True)
        S.op("sp", lambda e: e.dma_start(out=csg[0:NT, 0, :], in_=css_d), writes=[R("csg")], dma=True)
        xt = xring[0]
        rx = R("s1_0")
        S.op("sp", lambda e: e.dma_start(out=xt[0:NT, :], in_=xs), writes=[rx], dma=True)
        S.op("act", lambda e: e.activation(out=junk[0:NT, :], in_=xt[0:NT, :], func=AF.Square,
                                           accum_out=small[0:NT, 0:1]), reads=[rx], writes=[R("junk"), R("ss0")])
        rstd_ops(small[0:NT, 0:1], small[0:NT, 1:2], 1024.0, R("ss0"))
        S.op("dve", lambda e: e.tensor_scalar(out=hbf[0:NT, :], in0=xt[0:NT, :], scalar1=small[0:NT, 1:2],
                                              scalar2=None, op0=ALU.mult), reads=[rx, R("ss0")], writes=[*r_hbf])
        psT = bank(0).bitcast(BF16)
        for c in range(8):
            S.op("pe", lambda e, c=c: e.transpose(out=psT[:, c * NT:(c + 1) * NT],
                                                  in_=hbf[0:NT, c * 128:(c + 1) * 128], identity=ident_bf[0:NT, 0:NT]),
                 reads=[*r_hbf, R("ident_bf")], writes=[RB(0)])
        S.op("dve", lambda e: e.tensor_tensor(out=hT[:, :, 0:NT], in0=psT[:, 0:8 * NT].rearrange("p (c n) -> p c n", c=8),
                                              in1=bcast_last(gpre[:, 0, :], NT), op=ALU.mult),
             reads=[RB(0), R("gpre")], writes=[*r_hT])
        for n in range(3):
            for c in range(8):
                S.op("pe", lambda e, n=n, c=c: e.matmul(bank(1 + n)[0:NT, :], lhsT=hT[:, c, 0:NT],
                                                        rhs=w_in_sb[:, c, n * 512:(n + 1) * 512],
                                                        start=(c == 0), stop=(c == 7)),
                     reads=[*r_hT, R(f"w_in{n}")], writes=[RB(1 + n)])
        cosb = bcast_last(csg[0:NT, 0, 0:32], 8)
        sinb = bcast_last(csg[0:NT, 0, 32:64], 8)
        tA = ropeA[0:NT, :].rearrange("p (d h) -> p d h", h=8)
        tB = ropeB[0:NT, :].rearrange("p (d h) -> p d h", h=8)

        def rope(src_bank, dst, rdst):
            src = bank(src_bank)[0:NT, :].rearrange("p (h two d) -> p two d h", h=8, two=2)
            dv = dst[0:NT, :].rearrange("p (h two d) -> p two d h", h=8, two=2)
            rsrc = RB(src_bank)
            S.op("dve", lambda e: e.tensor_tensor(out=tA, in0=src[:, 0], in1=cosb, op=ALU.mult),
                 reads=[rsrc, R("csg")], writes=[*r_ropeA])
            S.op("dve", lambda e: e.tensor_tensor(out=tB, in0=src[:, 1], in1=sinb, op=ALU.mult),
                 reads=[rsrc, R("csg")], writes=[*r_ropeB])
            S.op("pool", lambda e: e.tensor_tensor(out=dv[:, 0], in0=tA, in1=tB, op=ALU.subtract),
                 reads=[*r_ropeA, *r_ropeB], writes=rdst)
            S.op("dve", lambda e: e.tensor_tensor(out=tA, in0=src[:, 1], in1=cosb, op=ALU.mult),
                 reads=[rsrc, R("csg")], writes=[*r_ropeA])
            S.op("dve", lambda e: e.tensor_tensor(out=tB, in0=src[:, 0], in1=sinb, op=ALU.mult),
                 reads=[rsrc, R("csg")], writes=[*r_ropeB])
            S.op("pool", lambda e: e.tensor_tensor(out=dv[:, 1], in0=tA, in1=tB, op=ALU.add),
                 reads=[*r_ropeA, *r_ropeB], writes=rdst)

        rope(1, q_r, r_qr)
        rope(2, k_r, r_kr)
        S.op("act", lambda e: e.activation(out=k_rb[0:NT, :], in_=k_r[0:NT, :], func=AF.Copy),
             reads=[R("s5a")], writes=[*r_krb])
        S.op("act", lambda e: e.activation(out=v_f[0:NT, :], in_=bank(3)[0:NT, :], func=AF.Copy),
             reads=[RB(3)], writes=[R("s5b")])
        vdst = Vnat[0:NT, 0, :].rearrange("p (h c) -> p h c", h=4)
        vsrc = v_f[0:NT, :].rearrange("p (h a d) -> p h a d", h=4, a=2)
        S.op("pool", lambda e: e.tensor_copy(out=vdst[:, :, 0:64], in_=vsrc[:, :, 0, :]),
             reads=[R("s5b")], writes=[R("Vnat")])
        S.op("pool", lambda e: e.tensor_copy(out=vdst[:, :, 128:192], in_=vsrc[:, :, 1, :]),
             reads=[R("s5b")], writes=[R("Vnat")])
        S.op("pool", lambda e: e.dma_start(out=ks, in_=k_r[0:NT, :]), reads=[R("s5a")], dma=True)
        S.op("pool", lambda e: e.dma_start(out=vs, in_=v_f[0:NT, :]), reads=[R("s5b")], dma=True)
        psQ = bank(4).bitcast(BF16)
        for c in range(4):
            S.op("pe", lambda e, c=c: e.transpose(out=psQ[:, c * NT:(c + 1) * NT], in_=q_r[0:NT, c * 128:(c + 1) * 128],
                                                  identity=ident_bf[0:NT, 0:NT]),
                 reads=[*r_qr, R("ident_bf")], writes=[RB(4)])
        for c in range(4):
            S.op("pe", lambda e, c=c: e.transpose(out=psQ[:, 512 + c * NT:512 + (c + 1) * NT],
                                                  in_=k_rb[0:NT, c * 128:(c + 1) * 128], identity=ident_bf[0:NT, 0:NT]),
                 reads=[*r_krb, R("ident_bf")], writes=[RB(4)])
        S.op("act", lambda e: e.activation(out=qT[:, :, 0:NT], in_=psQ[:, 0:4 * NT].rearrange("p (c n) -> p c n", c=4),
                                           func=AF.Copy), reads=[RB(4)], writes=[R("qT")])
        S.op("act", lambda e: e.activation(out=qT[:, :, NT:2 * NT],
                                           in_=psQ[:, 512:512 + 4 * NT].rearrange("p (c n) -> p c n", c=4),
                                           func=AF.Copy), reads=[RB(4)], writes=[R("qT")])
        scs = junk[:, :].bitcast(F32)
        S.op("sp", lambda e: e.dma_start(out=scs[0:32, :], in_=scv.rearrange("s t n -> (s t) n")),
             writes=[R("junk")], dma=True)
        giS = gi[:, :, 0:96].rearrange("p j (s u) -> p j s u", u=6)
        for j in range(4):
            S.op("pe", lambda e, j=j: e.transpose(out=bank(0)[:, j * 32:(j + 1) * 32], in_=scs[0:32, j * 128:(j + 1) * 128],
                                                  identity=ident_f[0:32, 0:32]),
                 reads=[R("junk"), R("ident_f")], writes=[RB(0)])
        for j in range(4):
            S.op("dve", lambda e, j=j: e.tensor_copy(out=giS[:, j, :, 0:2],
                                                     in_=bank(0)[:, j * 32:(j + 1) * 32].rearrange("p (s t) -> p s t", t=2)),
                 reads=[RB(0)], writes=[*r_gi])
        for j in range(4):
            for (bk, col0) in ((5, 2048), (6, 2560), (7, 1536)):
                for c in range(8):
                    S.op("pe", lambda e, bk=bk, col0=col0, c=c, j=j: e.matmul(
                        bank(bk)[:, 0:NT], lhsT=w_in_sb[:, c, col0 + j * 128:col0 + (j + 1) * 128],
                        rhs=hT[:, c, 0:NT], start=(c == 0), stop=(c == 7)),
                        reads=[*r_hT, R(f"w_in{col0 // 512}")], writes=[RB(bk)])
            S.op("act", lambda e: e.activation(out=u_sb[:, 0:NT], in_=bank(6)[:, 0:NT], func=AF.Copy),
                 reads=[RB(6)], writes=[*r_usb])
            S.op("dve", lambda e, j=j: e.tensor_tensor(out=giS[:, j, :, 2:6],
                                                       in0=bank(5)[:, 0:NT].rearrange("p (s t) -> p s t", t=4),
                                                       in1=u_sb[:, 0:NT].rearrange("p (s t) -> p s t", t=4), op=ALU.mult),
                 reads=[RB(5), *r_usb], writes=[*r_gi])
            t0v = t0[:, 0:NT].rearrange("p (s t) -> p s t", t=4)
            S.op("pool", lambda e, j=j: e.tensor_scalar(out=t0v, in0=giS[:, j, :, 0:4], scalar1=convw[:, j, 0:1],
                                                        scalar2=None, op0=ALU.mult),
                 reads=[*r_gi, R("convw")], writes=[*r_t0])
            for tap in (1, 2):
                S.op("dve", lambda e, j=j, tap=tap: e.scalar_tensor_tensor(
                    out=t0v, in0=giS[:, j, :, tap:tap + 4], scalar=convw[:, j, tap:tap + 1], in1=t0v,
                    op0=ALU.mult, op1=ALU.add), reads=[*r_gi, R("convw"), *r_t0], writes=[*r_t0])
            S.op("dve", lambda e, j=j: e.tensor_tensor(out=convT[:, j, 0:NT], in0=bank(7)[:, 0:NT], in1=t0[:, 0:NT],
                                                       op=ALU.mult), reads=[RB(7), *r_t0], writes=[R("convT")])
            S.op("act", lambda e, j=j: e.activation(out=sqA[:, 0:NT], in_=convT[:, j, 0:NT], func=AF.Square),
                 reads=[R("convT")], writes=r_s9b)
            S.op("pe", lambda e, j=j: e.matmul(bank(4)[:, 0:NT], lhsT=ones_bf[:, :], rhs=sqA[:, 0:NT],
                                               start=(j == 0), stop=(j == 3)),
                 reads=[*r_s9b, R("ones_bf")], writes=[RB(4)])
        S.op("act", lambda e: e.activation(out=rbA[:, 0:NT], in_=bank(4)[:, 0:NT], func=AF.Ln, scale=1.0 / 512,
                                           bias=epsc[:, :]), reads=[RB(4), R("epsc")], writes=r_s9b)
        S.op("act", lambda e: e.activation(out=rbA[:, 0:NT], in_=rbA[:, 0:NT], func=AF.Exp, scale=-0.5),
             reads=r_s9b, writes=r_s9b)
        for j in range(4):
            S.op("dve", lambda e, j=j: e.scalar_tensor_tensor(
                out=convT[:, j, 0:NT], in0=convT[:, j, 0:NT], scalar=gmix[:, 1, j:j + 1], in1=rbA[:, 0:NT],
                op0=ALU.mult, op1=ALU.mult), reads=[R("convT"), *r_s9b, R("gmix")], writes=[R("convT")])
        nst = t0[:, 0:128].rearrange("p (j s t) -> p j s t", j=4, t=2)
        for j in range(4):
            S.op("pool", lambda e, j=j: e.tensor_copy(out=nst[:, j], in_=giS[:, j, :, 4:6]),
                 reads=[*r_gi], writes=[*r_t0])
        for j in range(4):
            S.op("pe", lambda e, j=j: e.transpose(out=bank(0)[0:32, j * 128:(j + 1) * 128],
                                                  in_=t0[:, j * 32:(j + 1) * 32], identity=ident_f[:, :]),
                 reads=[*r_t0, R("ident_f")], writes=[RB(0)])
        S.op("act", lambda e: e.activation(out=scs[0:32, :], in_=bank(0)[0:32, :], func=AF.Copy),
             reads=[RB(0)], writes=[R("junk")])
        S.op("pool", lambda e: e.dma_start(out=cso.rearrange("s t n -> (s t) n"), in_=scs[0:32, :]),
             reads=[R("junk")], dma=True)

        wflat = w_in_sb[:, :, :].rearrange("p c n -> p (c n)")
        sets = [
            dict(Kc=kT[:, :, :].rearrange("p c n -> p (c n)")[:, 0:9 * 512].rearrange("p (x n) -> p x n", x=9),
                 KTs=Vr4[:, :, :].rearrange("p t n -> p (t n)")[:, 0:4 * 1152].rearrange("p (c n) -> p c n", c=4),
                 Vc=Vr16[:, 0:9, :], rK=R("kT"), rKT=R("Vr4"), rVc=R("Vr16"), P=P_s, rP=R("P_s")),
            dict(Kc=Vnat[:, :, :].rearrange("p t n -> p (t n)")[:, 768:768 + 9 * 512].rearrange("p (x n) -> p x n", x=9),
                 KTs=wflat[:, 0:4608].rearrange("p (c n) -> p c n", c=4),
                 Vc=wflat[:, 4608:4608 + 9 * 768].rearrange("p (x n) -> p x n", x=9),
                 rK=R("Kc1"), rKT=R("KTs1"), rVc=R("Vc1"), P=P_s2, rP=R("P_s2")),
        ]
        Vstage = wflat[:, 11520:20736].bitcast(F32).rearrange("p (x n) -> p x n", x=9)
        vc1 = sets[1]["Vc"].rearrange("p x (h c) -> p x h c", h=4)
        S.op("pool", lambda e: e.memset(vc1[:, :, :, 64:128], 1.0),
             writes=[*[R(f"w_in{j}") for j in range(6)], R("Vnat"), R("Kc1"), R("KTs1"), R("Vc1"), R("Vstage")])
        for h in range(8):
            rows = slice(64 * (h % 2), 64 * (h % 2) + 64)
            bk = 5 + (h % 2)
            S.op("pe", lambda e, h=h, rows=rows, bk=bk: e.matmul(
                bank(bk)[0:NT, (h // 2) * NT:(h // 2 + 1) * NT], lhsT=qT[rows, h // 2, NT:2 * NT],
                rhs=qT[rows, h // 2, 0:NT], start=True, stop=True), reads=[R("qT")], writes=[RB(bk)])
        pnv = P_new[:, :, :].rearrange("p (c a) n -> p c a n", a=2)
        for a in range(2):
            S.op("act", lambda e, a=a: e.activation(out=pnv[:, :, a, :],
                                                    in_=bank(5 + a)[0:NT, 0:4 * NT].rearrange("p (c n) -> p c n", c=4),
                                                    func=AF.Exp, scale=0.125), reads=[RB(5 + a)], writes=[R("P_new")])
        S.op("dve", lambda e: e.tensor_tensor(out=P_new[:, :, :], in0=P_new[:, :, :], in1=wnew8[:, :, :], op=ALU.mult),
             reads=[R("P_new"), R("wnew8")], writes=[R("P_new")])
        psOs = bank(7)

        def s_loadK(sq):
            st = sets[sq % 2]
            dst, rr = st["Kc"], st["rK"]
            S.op("pool", lambda e: e.dma_start(out=dst[:, 0, :], in_=ck[sq, 1920:2048, :]), writes=[rr], dma=True)
            S.op("pool", lambda e: e.dma_start(out=dst[:, 1:5, :],
                                               in_=ck[sq, 1536:2048, :].rearrange("(i t) n -> i t n", t=4)),
                 writes=[rr], dma=True)
            S.op("pool", lambda e: e.dma_start(out=dst[:, 5:9, :],
                                               in_=ck[sq].rearrange("(i r) n -> i r n", r=16)[:, 0:4, :]),
                 writes=[rr], dma=True)

        def s_loadV(sq):
            S.op("sp", lambda e: e.dma_start(out=Vstage[:, 0, :], in_=cv[sq, 1920:2048, :]), writes=[R("Vstage")], dma=True)
            S.op("sp", lambda e: e.dma_start(out=Vstage[:, 1:5, :],
                                             in_=cv[sq, 1536:2048, :].rearrange("(i t) n -> i t n", t=4)),
                 writes=[R("Vstage")], dma=True)
            S.op("sp", lambda e: e.dma_start(out=Vstage[:, 5:9, :],
                                             in_=cv[sq].rearrange("(i r) n -> i r n", r=16)[:, 0:4, :]),
                 writes=[R("Vstage")], dma=True)

        def s_prepV(sq):
            st = sets[sq % 2]
            Kc, KTs, Vc = st["Kc"], st["KTs"], st["Vc"]
            vcv = Vc.rearrange("p x (h c) -> p x h c", h=4)
            vsv = Vstage.rearrange("p x (h a d) -> p x h a d", h=4, a=2)
            for x in range(9):
                for a_ in range(2):
                    eng = "act" if (x + a_) % 2 == 0 else "dve"
                    dcol = slice(0, 64) if a_ == 0 else slice(128, 192)
                    if eng == "act":
                        S.op("act", lambda e, x=x, a_=a_, dcol=dcol: e.activation(out=vcv[:, x, :, dcol], in_=vsv[:, x, :, a_, :],
                                                                                 func=AF.Copy),
                             reads=[R("Vstage")], writes=[st["rVc"]])
                    else:
                        S.op("dve", lambda e, x=x, a_=a_, dcol=dcol: e.tensor_copy(out=vcv[:, x, :, dcol], in_=vsv[:, x, :, a_, :]),
                             reads=[R("Vstage")], writes=[st["rVc"]])

        def s_prepK(sq):
            st = sets[sq % 2]
            Kc, KTs = st["Kc"], st["KTs"]
            for x in range(9):
                pb = bank(x % 2).bitcast(BF16)
                for c in range(4):
                    S.op("pe", lambda e, x=x, c=c, pb=pb: e.transpose(out=pb[:, c * 128:(c + 1) * 128],
                                                                      in_=Kc[:, x, c * 128:(c + 1) * 128],
                                                                      identity=ident_bf[:, :]),
                         reads=[st["rK"], R("ident_bf")], writes=[RB(x % 2)])
                if x % 2 == 0:
                    S.op("act", lambda e, x=x, pb=pb: e.activation(
                        out=KTs[:, :, x * 128:(x + 1) * 128], in_=pb[:, 0:512].rearrange("p (c n) -> p c n", c=4),
                        func=AF.Copy), reads=[RB(x % 2)], writes=[st["rKT"]])
                else:
                    S.op("dve", lambda e, x=x, pb=pb: e.tensor_copy(
                        out=KTs[:, :, x * 128:(x + 1) * 128], in_=pb[:, 0:512].rearrange("p (c n) -> p c n", c=4)),
                        reads=[RB(x % 2)], writes=[st["rKT"]])

        def s_comp(sq):
            st = sets[sq % 2]
            KTs, Vc, rP = st["KTs"], st["Vc"], st["rP"]
            Pq = st["P"][:, :, :].rearrange("p h n -> p (h n)")
            sc = bank(2)
            for c in range(4):
                S.op("pe", lambda e, c=c: e.matmul(sc[:, 0:32], lhsT=KTs[:, c, 0:128], rhs=Qbd[:, c, sq * 32:(sq + 1) * 32],
                                                   start=(c == 0), stop=(c == 3)),
                     reads=[st["rKT"], R("Qbd")], writes=[RB(2)])
            for (x0, co) in ((1, 32), (5, 64)):
                for t in range(4):
                    for c in range(4):
                        S.op("pe", lambda e, c=c, t=t, x0=x0, co=co: e.matmul(
                            sc[:, co + t * 8:co + t * 8 + 8], lhsT=KTs[:, c, (x0 + t) * 128:(x0 + t + 1) * 128],
                            rhs=Qbd[:, c, sq * 32 + t:sq * 32 + 32:4], start=(c == 0), stop=(c == 3)),
                            reads=[st["rKT"], R("Qbd")], writes=[RB(2)])
            S.op("act", lambda e: e.activation(out=Pq[:, 0:96], in_=sc[:, 0:96], func=AF.Exp, scale=0.125),
                 reads=[RB(2)], writes=[rP])
            S.op("dve", lambda e: e.tensor_tensor(out=Pq[:, 0:32], in0=Pq[:, 0:32],
                                                  in1=smask8[:, :, :].rearrange("p h t -> p (h t)"), op=ALU.mult),
                 reads=[rP, R("smask8")], writes=[rP])
            for h in range(8):
                vcol = (h // 2) * 192 + 64 * (h % 2)
                oc = sq * 32 + h * 4
                S.op("pe", lambda e, h=h, vcol=vcol, oc=oc: e.matmul(
                    psOs[:, oc:oc + 4], lhsT=Vc[:, 0, vcol:vcol + 128], rhs=Pq[:, h * 4:h * 4 + 4], start=True, stop=False),
                    reads=[st["rVc"], rP], writes=[RB(7)])
                for t in range(4):
                    for (x0, co) in ((1, 32), (5, 64)):
                        S.op("pe", lambda e, h=h, vcol=vcol, oc=oc, t=t, x0=x0, co=co: e.matmul(
                            psOs[:, oc + t:oc + t + 1], lhsT=Vc[:, x0 + t, vcol:vcol + 128],
                            rhs=Pq[:, co + t * 8 + h:co + t * 8 + h + 1], start=False, stop=False),
                            reads=[st["rVc"], rP], writes=[RB(7)])
                S.op("pe", lambda e, h=h, vcol=vcol, oc=oc: e.matmul(
                    psOs[:, oc:oc + 4], lhsT=Vnat[0:NT, 0, vcol:vcol + 128], rhs=P_new[:, h, 4 * sq:4 * sq + 4],
                    start=False, stop=True), reads=[R("Vnat"), R("P_new")], writes=[RB(7)])

        Qbd = Vr16[:, 9:12, :].rearrange("p t n -> p (t n)")[:, 0:2048].rearrange("p (c n) -> p c n", c=4)
        S.op("pool", lambda e: e.memset(Qbd, 0.0), writes=[R("Qbd"), R("Vr16")])
        for c in range(4):
            for a_ in range(2):
                rows = slice(64 * a_, 64 * a_ + 64)
                dst = Qbd[rows, c, :].rearrange("p (s h t) -> p s h t", h=8, t=4)[:, :, 2 * c + a_, :]
                src = qT[rows, c, 0:NT].rearrange("p (s t) -> p s t", t=4)
                S.op("dve", lambda e, dst=dst, src=src: e.tensor_copy(out=dst, in_=src),
                     reads=[R("qT")], writes=[R("Qbd")])

        s_loadK(0)
        s_loadV(0)
        s_loadK(1)
        s_prepV(0)
        s_prepK(0)
        for sq in range(NS):
            if sq + 1 < NS:
                s_loadV(sq + 1)
                s_prepK(sq + 1)
            s_comp(sq)
            if sq + 1 < NS:
                s_prepV(sq + 1)
            if sq + 2 < NS:
                s_loadK(sq + 2)
        acc = accs[0]
        S.op("dve", lambda e: e.tensor_copy(out=acc, in_=psOs), reads=[RB(7)], writes=r_acc[0])
        accv = acc.rearrange("p (s c a t) -> p c a s t", s=NS, c=4, a=2)
        Rv = Rb[:, 0:256].rearrange("p (c s t) -> p c s t", c=4, t=4)
        for c in range(4):
            S.op("dve", lambda e, c=c: e.reciprocal(out=Rv[0:64, c], in_=accv[64:128, c, 0]),
                 reads=r_acc[0], writes=[*r_kr])
            S.op("dve", lambda e, c=c: e.reciprocal(out=Rv[64:128, c], in_=accv[0:64, c, 1]),
                 reads=r_acc[0], writes=[*r_kr])
        for c in range(4):
            S.op("dve", lambda e, c=c: e.tensor_tensor(
                out=attnT[0:64, c, 0:NT].rearrange("p (s t) -> p s t", t=4), in0=accv[0:64, c, 0], in1=Rv[0:64, c],
                op=ALU.mult), reads=[*r_acc[0], *r_kr], writes=[R("attnT")])
            S.op("dve", lambda e, c=c: e.tensor_tensor(
                out=attnT[64:128, c, 0:NT].rearrange("p (s t) -> p s t", t=4), in0=accv[64:128, c, 1], in1=Rv[64:128, c],
                op=ALU.mult), reads=[*r_acc[0], *r_kr], writes=[R("attnT")])
        for c in range(4):
            S.op("act", lambda e, c=c: e.activation(out=sqB[:, 0:NT], in_=attnT[:, c, 0:NT], func=AF.Square),
                 reads=[R("attnT")], writes=[*r_sqB])
            S.op("pe", lambda e, c=c: e.matmul(bank(4)[:, 0:NT], lhsT=ones_bf[:, :], rhs=sqB[:, 0:NT],
                                               start=(c == 0), stop=(c == 3)),
                 reads=[*r_sqB, R("ones_bf")], writes=[RB(4)])
        S.op("act", lambda e: e.activation(out=rbB[:, 0:NT], in_=bank(4)[:, 0:NT], func=AF.Ln, scale=1.0 / 512,
                                           bias=epsc[:, :]), reads=[RB(4), R("epsc")], writes=[*r_vf])
        S.op("act", lambda e: e.activation(out=rbB[:, 0:NT], in_=rbB[:, 0:NT], func=AF.Exp, scale=-0.5),
             reads=[*r_vf], writes=[*r_vf])
        for c in range(4):
            S.op("dve", lambda e, c=c: e.scalar_tensor_tensor(
                out=attnT[:, c, 0:NT], in0=attnT[:, c, 0:NT], scalar=gmix[:, 0, c:c + 1], in1=rbB[:, 0:NT],
                op0=ALU.mult, op1=ALU.mult), reads=[R("attnT"), *r_vf, R("gmix")], writes=[R("attnT")])
        for fn in mlp_block(1, NT, lambda t: xs, lambda t: ys,
                            lambda c, t: (attnT if c < 4 else convT)[:, c % 4, 0:NT], NT):
            fn()

    for b in range(NSEQ):
        for g in range(4):
            if dbg is not None and (b, g) not in dbg["bg"]:
                continue
            ph = "ABC" if dbg is None else dbg["ph"]
            if "A" in ph:
                phase_A(b, g)
            if "B" in ph:
                phase_B(b, g)
            if "C" in ph:
                finish_conv()
                phase_C(b, g)
    if with_sample and (dbg is None or dbg.get("sample")):
        finish_conv()
        sample_all()

    if dbg is not None and dbg.get("dump") == "csg":
        S.op("pool", lambda e: e.dma_start(out=dbgo[:, 0:256], in_=csg[:, :, :].rearrange("p t c -> p (t c)")),
             reads=[R("csg")], dma=True)
    with ExitStack() as es2:
        sems = {e: es2.enter_context(nc.semaphore(f"sem_{e}")) for e in ENGS}
        dsems = {}
        for q in ("sp", "pool"):
            for k in range(S.ndma):
                dsems[(q, k)] = es2.enter_context(nc.semaphore(f"dsem_{q}{k}"))
        with nc.Block() as block:
            S.emit(nc, block, sems, dsems)
    es.close()
    return nc


def _host_consts():
    p = np.arange(128)[:, None]
    j = np.arange(128)[None, :]
    prev = np.where(j <= p, 0.0, NEG).astype(np.float32)
    diag = np.where(p <= j, 0.0, NEG).astype(np.float32)
    neg = np.full((128, 128), NEG, np.float32)
    v0 = np.concatenate([prev, diag, prev, diag], 1)
    v1 = np.concatenate([neg, diag, prev, diag], 1)
    v2 = np.concatenate([neg, diag, neg, diag], 1)
    mask12 = np.concatenate([v0, v1, v2], 1)
    m3 = []
    jj = np.arange(32)[None, :]
    for g in range(4):
        blk = np.where(p <= 32 * g + jj, 0.0, NEG).astype(np.float32)
        m3.append(np.tile(blk, (1, 16)))
    mask3 = np.concatenate(m3, 1)
    half = 32
    inv = (10000.0 ** (-np.arange(half, dtype=np.float32) * 2.0 / 64)).astype(np.float32)
    pos = np.arange(SEQ, dtype=np.float32)
    ang = pos[:, None] * inv[None, :]
    cs = np.concatenate([np.cos(ang), np.sin(ang)], 1).astype(np.float32)
    csp = cs.reshape(16, 128, 64).transpose(1, 0, 2).reshape(128, 16 * 64)
    poss = (PAST + np.arange(4)).astype(np.float32)
    angs = poss[:, None] * inv[None, :]
    css1 = np.concatenate([np.cos(angs), np.sin(angs)], 1).astype(np.float32)
    css = np.tile(css1, (NS, 1))
    smask = np.where(p >= np.arange(4)[None, :], 1.0, 0.0).astype(np.float32)
    wnew = np.zeros((64, 64), np.float32)
    for s in range(NS):
        for t in range(4):
            for t2 in range(t + 1):
                wnew[4 * s + t2, 4 * s + t] = 3.0 if t2 == t else 1.0
    smask = np.ascontiguousarray(np.tile(smask, (1, 8)))
    wnew = np.ascontiguousarray(np.tile(wnew, (1, 8)))
    return dict(ident=np.eye(128, dtype=np.float32), mask12=np.ascontiguousarray(mask12),
                mask3=np.ascontiguousarray(mask3), csp=np.ascontiguousarray(csp), css=css,
                smask=smask, wnew=wnew)


def kernel(x_prompt, x_sample, cache_k, cache_v, state_conv, n_att_pre, n_att_post, w_in, conv_w,
           g_att, g_conv, w_out, n_mlp_pre, n_mlp_post, w_up, w_down):
    f = lambda a: np.ascontiguousarray(np.asarray(a, dtype=np.float32))
    x_prompt, x_sample, cache_k, cache_v, state_conv = map(f, (x_prompt, x_sample, cache_k, cache_v, state_conv))
    consts = _host_consts()
    col = lambda v: f(v).reshape(-1, 128).T
    gpre = np.concatenate([col(n_att_pre[0]), col(n_mlp_pre[0])], 1)
    gpost = np.concatenate([np.broadcast_to(f(n_att_post[0])[None, :], (128, D)),
                            np.broadcast_to(f(n_mlp_post[0])[None, :], (128, D))], 1)
    gmix = np.concatenate([col(g_att[0]), col(g_conv[0])], 1)
    cw = f(conv_w[0])
    convw = cw.reshape(3, 4, 128).transpose(2, 1, 0).reshape(128, 12)
    shared = dict(w_in=f(w_in[0]), w_out=f(w_out[0]), w_up=f(w_up[0]), w_down=f(w_down[0]),
                  gpre=f(gpre), gpost=f(gpost), gmix=f(gmix), convw=f(convw), **consts)
    in_maps = []
    for c in range(NCORES):
        m = dict(shared)
        m["xp"] = x_prompt[NSEQ * c:NSEQ * (c + 1)]
        m["xs"] = x_sample[NS * c:NS * (c + 1)].reshape(64, D)
        m["ck"] = cache_k[0, NS * c:NS * (c + 1)].reshape(NS, 2048, 512)
        m["cv"] = cache_v[0, NS * c:NS * (c + 1)].reshape(NS, 2048, 512)
        m["scv"] = state_conv[0, NS * c:NS * (c + 1)]
        in_maps.append(m)
    nc = build()
    res = run_bass_kernel_spmd(nc, in_maps, core_ids=list(range(NCORES)))
    rs = res.results
    cat = lambda k: np.concatenate([np.asarray(r[k], dtype=np.float32) for r in rs], 0)
    y_p = cat("yp")
    y_s = cat("ys").reshape(128, 4, D)
    k_p = cat("kp").reshape(1, 16, SEQ, 8, 64)
    v_p = cat("vp").reshape(1, 16, SEQ, 8, 64)
    c_p = cat("cp").reshape(1, 16, 2, 512)
    k_s = cat("ks").reshape(1, 128, 4, 8, 64)
    v_s = cat("vs").reshape(1, 128, 4, 8, 64)
    c_s = cat("cso").reshape(1, 128, 2, 512)
    return (y_p, y_s, k_p, v_p, c_p, k_s, v_s, c_s)
```
